# Optimizing a Trainium2 kernel written in Bass

```python
import jax, jax.numpy as jnp
from jax import lax
import numpy as np

D_MODEL = 2048
BATCH = 1
SEQ = 8192
DEPTH = 2
DEC_BATCH = 8
DEC_SEQ = 32
PAST_LEN = 1024

CHUNK = 64
Q_BLOCK = 128
N_AB = (DEPTH + 1) // 2
N_SSD = DEPTH // 2
RMS_EPS = 1e-6
D_FF = 5632

GLA_HEADS = 4
GLA_DK = 128
GLA_DV = 256
GLA_GATE_RANK = 16
GLA_TAU = 16.0
MLA_HEADS = 8
MLA_Q_LORA = 512
MLA_KV_LORA = 512
MLA_NOPE = 128
MLA_ROPE = 64
MLA_V = 128
ROPE_THETA = 10000.0
SSD_D_INNER = 2 * D_MODEL
SSD_HEADDIM = 64
SSD_HEADS = SSD_D_INNER // SSD_HEADDIM
SSD_GROUPS = 8
SSD_REP = SSD_HEADS // SSD_GROUPS
SSD_N = 128
SSD_CONV = 4
SSD_CONV_CH = SSD_D_INNER + 2 * SSD_GROUPS * SSD_N

GLA_QK = GLA_HEADS * GLA_DK
GLA_VW = GLA_HEADS * GLA_DV
MLA_VW = MLA_HEADS * MLA_V
MLA_QW = MLA_HEADS * (MLA_NOPE + MLA_ROPE)
AB_IN = 2 * GLA_QK + 2 * GLA_VW + GLA_GATE_RANK + MLA_Q_LORA + MLA_KV_LORA + MLA_ROPE
AB_OUT = GLA_VW + MLA_VW
SSD_IN = SSD_D_INNER + SSD_CONV_CH + SSD_HEADS

kernel_name = 'hybrid_gla_mla_ssd_streaming_step'


def rmsnorm(x, g):
    xf = x.astype(jnp.float32)
    y = xf * lax.rsqrt(jnp.mean(xf * xf, axis=-1, keepdims=True) + RMS_EPS)
    return (y * g.astype(jnp.float32)).astype(x.dtype)


def swiglu(x, w_gate, w_up, w_down):
    return (jax.nn.silu(x @ w_gate) * (x @ w_up)) @ w_down


def rope(x, pos):
    half = MLA_ROPE // 2
    inv = ROPE_THETA ** (-jnp.arange(half, dtype=jnp.float32) / half)
    ang = pos.astype(jnp.float32)[:, None] * inv[None, :]
    shape = (ang.shape[0],) + (1,) * (x.ndim - 3) + (half,)
    cos = jnp.cos(ang).reshape(shape)
    sin = jnp.sin(ang).reshape(shape)
    x1 = x[..., :half].astype(jnp.float32)
    x2 = x[..., half:].astype(jnp.float32)
    return jnp.concatenate([x1 * cos - x2 * sin, x2 * cos + x1 * sin], axis=-1).astype(x.dtype)


def gla_recurrence(q, k, v, log_a, s0):
    B, T, H, DK = q.shape
    DV = v.shape[-1]
    L = min(CHUNK, T)
    n = T // L
    f32 = jnp.float32

    def blocks(a):
        return jnp.moveaxis(a.astype(f32).reshape((B, n, L) + a.shape[2:]), 1, 0)

    causal = jnp.tril(jnp.ones((L, L), dtype=bool))[None, :, :, None, None]

    def step(S, inp):
        qc, kc, vc, gc = inp
        b = jnp.cumsum(gc, axis=1)
        decay = jnp.exp(jnp.where(causal, b[:, :, None] - b[:, None, :], -jnp.inf))
        att = jnp.einsum('bthk,bshk,btshk->bhts', qc, kc, decay)
        o = jnp.einsum('bhts,bshv->bthv', att, vc) + jnp.einsum('bthk,bhkv->bthv', qc * jnp.exp(b), S)
        b_last = b[:, -1]
        S = S * jnp.exp(b_last)[..., None] + jnp.einsum('bshk,bshv->bhkv', kc * jnp.exp(b_last[:, None] - b), vc)
        return S, o

    S, o = lax.scan(step, s0.astype(f32), (blocks(q), blocks(k), blocks(v), blocks(log_a)))
    o = jnp.moveaxis(o, 0, 1).reshape(B, T, H, DV)
    return o.astype(v.dtype), S.astype(s0.dtype)


def ssd_recurrence(x, dt, a, bm, cm, s0):
    B, T = x.shape[:2]
    L = min(CHUNK, T)
    n = T // L
    f32 = jnp.float32

    def blocks(z):
        return jnp.moveaxis(z.astype(f32).reshape((B, n, L) + z.shape[2:]), 1, 0)

    causal = jnp.tril(jnp.ones((L, L), dtype=bool))[None, :, :, None, None]
    a32 = a.astype(f32)

    def step(S, inp):
        xc, dtc, bc, cc = inp
        cum = jnp.cumsum(dtc * a32, axis=1)
        lmat = jnp.exp(jnp.where(causal, cum[:, :, None] - cum[:, None, :], -jnp.inf))
        cb = jnp.einsum('btgn,bsgn->btsg', cc, bc)
        w = cb[..., None] * lmat * dtc[:, None]
        y = jnp.einsum('btsgr,bsgrp->btgrp', w, xc)
        y = y + jnp.einsum('btgn,bgrpn->btgrp', cc, S) * jnp.exp(cum)[..., None]
        last = cum[:, -1]
        wdec = jnp.exp(last[:, None] - cum) * dtc
        S = S * jnp.exp(last)[..., None, None] + jnp.einsum('bsgn,bsgrp->bgrpn', bc, xc * wdec[..., None])
        return S, y

    S0 = s0.astype(f32).reshape(B, SSD_GROUPS, SSD_REP, SSD_HEADDIM, SSD_N)
    S, y = lax.scan(step, S0, (blocks(x), blocks(dt), blocks(bm), blocks(cm)))
    y = jnp.moveaxis(y, 0, 1).reshape(x.shape)
    return y.astype(x.dtype), S.reshape(B, SSD_HEADS, SSD_HEADDIM, SSD_N).astype(s0.dtype)


def causal_dwconv(x, buf, w, b):
    T = x.shape[1]
    xp = jnp.concatenate([buf.astype(x.dtype), x], axis=1)
    y = b
    for j in range(SSD_CONV):
        y = y + xp[:, j:j + T] * w[j]
    return y, xp[:, -(SSD_CONV - 1):]


def mla_attention(q_nope, q_pe, k_nope, k_pe, v, q_pos, k_pos):
    B, Tq, H, _ = q_nope.shape
    blk = min(Q_BLOCK, Tq)
    n = Tq // blk
    scale = (MLA_NOPE + MLA_ROPE) ** -0.5
    k_chunk = k_pos // CHUNK

    def one_block(args):
        qn, qp, qpos = args
        s = jnp.einsum('bqhd,bkhd->bhqk', qn, k_nope) + jnp.einsum('bqhr,bkr->bhqk', qp, k_pe)
        s = s.astype(jnp.float32) * scale
        mask = k_chunk[None, :] <= (qpos // CHUNK)[:, None]
        p = jax.nn.softmax(jnp.where(mask[None, None], s, -jnp.inf), axis=-1).astype(v.dtype)
        return jnp.einsum('bhqk,bkhv->bqhv', p, v)

    qn_b = jnp.moveaxis(q_nope.reshape(B, n, blk, H, MLA_NOPE), 1, 0)
    qp_b = jnp.moveaxis(q_pe.reshape(B, n, blk, H, MLA_ROPE), 1, 0)
    out = lax.map(one_block, (qn_b, qp_b, q_pos.reshape(n, blk)))
    return jnp.moveaxis(out, 0, 1).reshape(B, Tq, H, MLA_V)


def ab_mixer(h, pos, gla_s0, ckv_past, kpe_past, w_in, w_gate_up, b_gate, g_gla, g_qn, w_uq, g_kvn, w_uk, w_uv, w_out):
    B, T, _ = h.shape
    sizes = (GLA_QK, GLA_QK, GLA_VW, GLA_GATE_RANK, GLA_VW, MLA_Q_LORA, MLA_KV_LORA, MLA_ROPE)
    offs = [int(o) for o in np.cumsum(sizes)[:-1]]
    q, k, v, gr, og, cq, ckv, kpe = jnp.split(h @ w_in, offs, axis=-1)
    q = q.reshape(B, T, GLA_HEADS, GLA_DK) * (GLA_DK ** -0.5)
    k = k.reshape(B, T, GLA_HEADS, GLA_DK)
    v = v.reshape(B, T, GLA_HEADS, GLA_DV)
    log_a = jax.nn.log_sigmoid((gr @ w_gate_up + b_gate).astype(jnp.float32)) / GLA_TAU
    log_a = log_a.reshape(B, T, GLA_HEADS, GLA_DK)
    o_gla, gla_s = gla_recurrence(q, k, v, log_a, gla_s0)
    o_gla = rmsnorm(o_gla, g_gla) * jax.nn.silu(og).reshape(B, T, GLA_HEADS, GLA_DV)
    qf = (rmsnorm(cq, g_qn) @ w_uq).reshape(B, T, MLA_HEADS, MLA_NOPE + MLA_ROPE)
    q_nope = qf[..., :MLA_NOPE]
    q_pe = rope(qf[..., MLA_NOPE:], pos)
    ckv_n = rmsnorm(ckv, g_kvn)
    kpe_r = rope(kpe, pos)
    if ckv_past is None:
        ckv_all, kpe_all, k_pos = ckv_n, kpe_r, pos
    else:
        ckv_all = jnp.concatenate([ckv_past.astype(ckv_n.dtype), ckv_n], axis=1)
        kpe_all = jnp.concatenate([kpe_past.astype(kpe_r.dtype), kpe_r], axis=1)
        k_pos = jnp.arange(ckv_all.shape[1], dtype=jnp.int32)
    Tk = ckv_all.shape[1]
    k_nope = (ckv_all @ w_uk).reshape(B, Tk, MLA_HEADS, MLA_NOPE)
    v_m = (ckv_all @ w_uv).reshape(B, Tk, MLA_HEADS, MLA_V)
    o_mla = mla_attention(q_nope, q_pe, k_nope, kpe_all, v_m, pos, k_pos)
    merged = jnp.concatenate([o_gla.reshape(B, T, GLA_VW), o_mla.reshape(B, T, MLA_VW)], axis=-1)
    return merged @ w_out, gla_s, ckv_n, kpe_r


def ssd_mixer(h, conv_buf, s0, w_in, w_conv, b_conv, dt_bias, a_log, d_skip, g_norm, w_out):
    B, T, _ = h.shape
    z, xbc, dt = jnp.split(h @ w_in, [SSD_D_INNER, SSD_D_INNER + SSD_CONV_CH], axis=-1)
    xbc, new_buf = causal_dwconv(xbc, conv_buf, w_conv, b_conv)
    xbc = jax.nn.silu(xbc)
    x, bm, cm = jnp.split(xbc, [SSD_D_INNER, SSD_D_INNER + SSD_GROUPS * SSD_N], axis=-1)
    x = x.reshape(B, T, SSD_GROUPS, SSD_REP, SSD_HEADDIM)
    bm = bm.reshape(B, T, SSD_GROUPS, SSD_N)
    cm = cm.reshape(B, T, SSD_GROUPS, SSD_N)
    dt = jax.nn.softplus((dt + dt_bias).astype(jnp.float32)).reshape(B, T, SSD_GROUPS, SSD_REP)
    a = -jnp.exp(a_log.astype(jnp.float32)).reshape(SSD_GROUPS, SSD_REP)
    y, s_new = ssd_recurrence(x, dt, a, bm, cm, s0)
    y = y + x * d_skip.reshape(SSD_GROUPS, SSD_REP)[..., None]
    y = (y.reshape(B, T, SSD_D_INNER) * jax.nn.silu(z)).reshape(B, T, SSD_GROUPS, SSD_D_INNER // SSD_GROUPS)
    y = rmsnorm(y, g_norm.reshape(SSD_GROUPS, SSD_D_INNER // SSD_GROUPS)).reshape(B, T, SSD_D_INNER)
    return y @ w_out, s_new, new_buf


def trunk(x, pos, past, P):
    B = x.shape[0]
    dt = x.dtype
    new_ckv, new_kpe, new_gla, new_ssd, new_conv = [], [], [], [], []
    for layer in range(DEPTH):
        h = rmsnorm(x, P['g_ffn1'][layer])
        x = x + 0.5 * swiglu(h, P['w_ffn1_gate'][layer], P['w_ffn1_up'][layer], P['w_ffn1_down'][layer])
        h = rmsnorm(x, P['g_mix'][layer])
        i = layer // 2
        if layer % 2 == 0:
            if past is None:
                gla0 = jnp.zeros((B, GLA_HEADS, GLA_DK, GLA_DV), dt)
                ckv_p, kpe_p = None, None
            else:
                gla0, ckv_p, kpe_p = past['gla'][i], past['ckv'][i], past['kpe'][i]
            out, gla_s, ckv_n, kpe_r = ab_mixer(
                h, pos, gla0, ckv_p, kpe_p, P['w_ab_in'][i], P['w_gla_gate_up'][i], P['b_gla_gate'][i],
                P['g_gla_norm'][i], P['g_mla_q_norm'][i], P['w_mla_uq'][i], P['g_mla_kv_norm'][i],
                P['w_mla_uk'][i], P['w_mla_uv'][i], P['w_ab_out'][i])
            new_gla.append(gla_s)
            new_ckv.append(ckv_n)
            new_kpe.append(kpe_r)
        else:
            if past is None:
                ssd0 = jnp.zeros((B, SSD_HEADS, SSD_HEADDIM, SSD_N), dt)
                buf0 = jnp.zeros((B, SSD_CONV - 1, SSD_CONV_CH), dt)
            else:
                ssd0, buf0 = past['ssd'][i], past['conv'][i]
            out, ssd_s, buf = ssd_mixer(
                h, buf0, ssd0, P['w_ssd_in'][i], P['w_ssd_conv'][i], P['b_ssd_conv'][i], P['ssd_dt_bias'][i],
                P['ssd_a_log'][i], P['ssd_d'][i], P['g_ssd_norm'][i], P['w_ssd_out'][i])
            new_ssd.append(ssd_s)
            new_conv.append(buf)
        x = x + out
        h = rmsnorm(x, P['g_ffn2'][layer])
        x = x + 0.5 * swiglu(h, P['w_ffn2_gate'][layer], P['w_ffn2_up'][layer], P['w_ffn2_down'][layer])
    y = rmsnorm(x, P['g_final'])
    return y, jnp.stack(new_ckv), jnp.stack(new_kpe), jnp.stack(new_gla), jnp.stack(new_ssd), jnp.stack(new_conv)


def setup_inputs(seed: int = 0) -> dict:
    key = jax.random.key(seed)
    ks = iter(jax.random.split(key, 64))
    f32 = jnp.float32

    def nrm(shape, scale):
        return jax.random.normal(next(ks), shape, f32) * scale

    def gain(shape):
        return 1.0 + 0.02 * jax.random.normal(next(ks), shape, f32)

    d = D_MODEL
    dt0 = jnp.exp(jax.random.uniform(next(ks), (N_SSD, SSD_HEADS), f32, np.log(1e-3), np.log(1e-1)))
    return {
        'x_prompt': nrm((BATCH, SEQ, d), 1.0),
        'x_sample': nrm((DEC_BATCH, DEC_SEQ, d), 1.0),
        'cache_mla_ckv': nrm((N_AB, DEC_BATCH, PAST_LEN, MLA_KV_LORA), 1.0),
        'cache_mla_kpe': nrm((N_AB, DEC_BATCH, PAST_LEN, MLA_ROPE), 1.0),
        'state_gla': nrm((N_AB, DEC_BATCH, GLA_HEADS, GLA_DK, GLA_DV), 0.1),
        'state_ssd': nrm((N_SSD, DEC_BATCH, SSD_HEADS, SSD_HEADDIM, SSD_N), 0.1),
        'state_ssd_conv': nrm((N_SSD, DEC_BATCH, SSD_CONV - 1, SSD_CONV_CH), 1.0),
        'g_ffn1': gain((DEPTH, d)),
        'w_ffn1_gate': nrm((DEPTH, d, D_FF), d ** -0.5),
        'w_ffn1_up': nrm((DEPTH, d, D_FF), d ** -0.5),
        'w_ffn1_down': nrm((DEPTH, D_FF, d), D_FF ** -0.5),
        'g_mix': gain((DEPTH, d)),
        'g_ffn2': gain((DEPTH, d)),
        'w_ffn2_gate': nrm((DEPTH, d, D_FF), d ** -0.5),
        'w_ffn2_up': nrm((DEPTH, d, D_FF), d ** -0.5),
        'w_ffn2_down': nrm((DEPTH, D_FF, d), D_FF ** -0.5),
        'w_ab_in': nrm((N_AB, d, AB_IN), d ** -0.5),
        'w_gla_gate_up': nrm((N_AB, GLA_GATE_RANK, GLA_QK), GLA_GATE_RANK ** -0.5),
        'b_gla_gate': nrm((N_AB, GLA_QK), 0.1) + 2.0,
        'g_gla_norm': gain((N_AB, GLA_DV)),
        'g_mla_q_norm': gain((N_AB, MLA_Q_LORA)),
        'w_mla_uq': nrm((N_AB, MLA_Q_LORA, MLA_QW), MLA_Q_LORA ** -0.5),
        'g_mla_kv_norm': gain((N_AB, MLA_KV_LORA)),
        'w_mla_uk': nrm((N_AB, MLA_KV_LORA, MLA_HEADS * MLA_NOPE), MLA_KV_LORA ** -0.5),
        'w_mla_uv': nrm((N_AB, MLA_KV_LORA, MLA_VW), MLA_KV_LORA ** -0.5),
        'w_ab_out': nrm((N_AB, AB_OUT, d), AB_OUT ** -0.5),
        'w_ssd_in': nrm((N_SSD, d, SSD_IN), d ** -0.5),
        'w_ssd_conv': nrm((N_SSD, SSD_CONV, SSD_CONV_CH), SSD_CONV ** -0.5),
        'b_ssd_conv': nrm((N_SSD, SSD_CONV_CH), 0.02),
        'ssd_dt_bias': dt0 + jnp.log(-jnp.expm1(-dt0)),
        'ssd_a_log': jnp.log(jax.random.uniform(next(ks), (N_SSD, SSD_HEADS), f32, 1.0, 16.0)),
        'ssd_d': gain((N_SSD, SSD_HEADS)),
        'g_ssd_norm': gain((N_SSD, SSD_D_INNER)),
        'w_ssd_out': nrm((N_SSD, SSD_D_INNER, d), SSD_D_INNER ** -0.5),
        'g_final': gain((d,)),
    }


def reference(x_prompt, x_sample, cache_mla_ckv, cache_mla_kpe, state_gla, state_ssd, state_ssd_conv,
              g_ffn1, w_ffn1_gate, w_ffn1_up, w_ffn1_down, g_mix, g_ffn2, w_ffn2_gate, w_ffn2_up, w_ffn2_down,
              w_ab_in, w_gla_gate_up, b_gla_gate, g_gla_norm, g_mla_q_norm, w_mla_uq, g_mla_kv_norm,
              w_mla_uk, w_mla_uv, w_ab_out, w_ssd_in, w_ssd_conv, b_ssd_conv, ssd_dt_bias, ssd_a_log, ssd_d,
              g_ssd_norm, w_ssd_out, g_final):
    P = dict(g_ffn1=g_ffn1, w_ffn1_gate=w_ffn1_gate, w_ffn1_up=w_ffn1_up, w_ffn1_down=w_ffn1_down,
             g_mix=g_mix, g_ffn2=g_ffn2, w_ffn2_gate=w_ffn2_gate, w_ffn2_up=w_ffn2_up, w_ffn2_down=w_ffn2_down,
             w_ab_in=w_ab_in, w_gla_gate_up=w_gla_gate_up, b_gla_gate=b_gla_gate, g_gla_norm=g_gla_norm,
             g_mla_q_norm=g_mla_q_norm, w_mla_uq=w_mla_uq, g_mla_kv_norm=g_mla_kv_norm, w_mla_uk=w_mla_uk,
             w_mla_uv=w_mla_uv, w_ab_out=w_ab_out, w_ssd_in=w_ssd_in, w_ssd_conv=w_ssd_conv,
             b_ssd_conv=b_ssd_conv, ssd_dt_bias=ssd_dt_bias, ssd_a_log=ssd_a_log, ssd_d=ssd_d,
             g_ssd_norm=g_ssd_norm, w_ssd_out=w_ssd_out, g_final=g_final)
    pos_p = jnp.arange(x_prompt.shape[1], dtype=jnp.int32)
    y_prompt, p_ckv, p_kpe, p_gla, p_ssd, p_conv = trunk(x_prompt, pos_p, None, P)
    past = dict(ckv=cache_mla_ckv, kpe=cache_mla_kpe, gla=state_gla, ssd=state_ssd, conv=state_ssd_conv)
    pos_s = cache_mla_ckv.shape[2] + jnp.arange(x_sample.shape[1], dtype=jnp.int32)
    y_sample, s_ckv, s_kpe, s_gla, s_ssd, s_conv = trunk(x_sample, pos_s, past, P)
    return (y_prompt, y_sample, p_ckv, p_kpe, p_gla, p_ssd, p_conv, s_ckv, s_kpe, s_gla, s_ssd, s_conv)
```

```python
import os
import types
import contextlib
import numpy as np
import concourse.bass as bass
import concourse.mybir as mybir
from concourse.bass_utils import run_bass_kernel_spmd

F32 = mybir.dt.float32
BF16 = mybir.dt.bfloat16
ALU = mybir.AluOpType
AF = mybir.ActivationFunctionType
AX = mybir.AxisListType

NCORES = 8
D = 2048
DFF = 5632
TP = 1024
TS = 32
T = TP + TS
NT = 9
EPS = 1e-6
ENGS = ("pe", "act", "dve", "pool", "sp")
C_TRII = 0
C_TRIX = 128
C_COS = 256
C_SIN = 544
C_CMASK = 832
C_RBIAS = 840
C_N16 = 849
C_ONE = 851
C_SEL = 852
CW = 866


def rows(i):
    return 128 if i < 8 else TS


def _freeze(fn):
    if fn.__closure__ is None:
        return fn
    cells = []
    for c in fn.__closure__:
        try:
            cells.append(types.CellType(c.cell_contents))
        except ValueError:
            cells.append(c)
    return types.FunctionType(fn.__code__, fn.__globals__, fn.__name__, fn.__defaults__, tuple(cells))


class Prog:
    def __init__(self, nc):
        self.nc = nc
        self.ops = []

    def op(self, eng, fn, reads=(), writes=(), signal=True):
        self.ops.append(dict(kind="c", eng=eng, fn=_freeze(fn), reads=tuple(reads), writes=tuple(writes), signal=signal))

    def dma(self, eng, fns, key, reads=(), writes=(), inc=16):
        self.ops.append(dict(kind="d", eng=eng, fns=[_freeze(f) for f in fns], key=key, reads=tuple(reads), writes=tuple(writes), inc=inc))

    def fence(self):
        self.ops.append(dict(kind="f"))

    def build(self, final_wait_eng="sp"):
        nc = self.nc
        ops = self.ops
        eng_count = {e: 0 for e in ENGS}
        key_count = {}
        pending = {e: [] for e in ENGS}
        for o in ops:
            if o["kind"] == "f":
                o["snap"] = ([(("e", e), eng_count[e] + len(pending[e])) for e in ENGS], [(("k", k), v) for k, v in key_count.items()])
                assert all(not pending[e] for e in ENGS), "fence inside unsignaled group"
                continue
            if o["kind"] == "c":
                e = o["eng"]
                if o["signal"]:
                    eng_count[e] += 1
                    o["done"] = ("e", e, eng_count[e])
                    for p in pending[e]:
                        p["done"] = o["done"]
                    pending[e] = []
                else:
                    pending[e].append(o)
            else:
                k = o["key"]
                key_count[k] = key_count.get(k, 0) + o["inc"] * len(o["fns"])
                o["done"] = ("k", k, key_count[k])
        for e in ENGS:
            assert not pending[e], f"trailing unsignaled ops on {e}"
        last_w, readers = {}, {}
        known = {e: {} for e in ENGS}
        fence_need = {e: {} for e in ENGS}
        for o in ops:
            if o["kind"] == "f":
                for e in ENGS:
                    for kk, v in o["snap"][0] + o["snap"][1]:
                        if v and not (kk == ("e", "pe") and e == "pe"):
                            fence_need[e][kk] = max(fence_need[e].get(kk, 0), v)
                continue
            e = o["eng"]
            need = dict(fence_need[e])
            fence_need[e] = {}

            def want(dep, is_raw):
                d = dep["done"]
                if dep["kind"] == "c" and dep["eng"] == e and e == "pe":
                    return
                kk = (d[0], d[1])
                if need.get(kk, 0) < d[2]:
                    need[kk] = d[2]

            for r in o["reads"]:
                if r in last_w:
                    want(last_w[r], True)
            for w in o["writes"]:
                if w in last_w:
                    want(last_w[w], True)
                for rd in readers.get(w, ()):
                    want(rd, False)
            waits = []
            for kk, v in need.items():
                if known[e].get(kk, 0) < v:
                    known[e][kk] = v
                    waits.append((kk, v))
            o["waits"] = waits
            for r in o["reads"]:
                readers.setdefault(r, []).append(o)
            for w in o["writes"]:
                last_w[w] = o
                readers[w] = []
        sems = {}
        with contextlib.ExitStack() as st:
            for e in ENGS:
                if eng_count[e]:
                    sems[("e", e)] = st.enter_context(nc.semaphore(f"s_{e}"))
            for i, k in enumerate(key_count):
                sems[("k", k)] = st.enter_context(nc.semaphore(f"d{i}"))
            finals = [(("k", k), v) for k, v in key_count.items()]
            finals += [(("e", e), v) for e, v in eng_count.items() if v]

            def emit(engname, handle):
                for o in ops:
                    if o["kind"] == "f" or o["eng"] != engname:
                        continue
                    for kk, v in o["waits"]:
                        handle.wait_ge(sems[kk], v)
                    if o["kind"] == "c":
                        ins = o["fn"](handle)
                        if o["signal"]:
                            ins.then_inc(sems[("e", engname)], 1)
                    else:
                        for f in o["fns"]:
                            f(handle).then_inc(sems[("k", o["key"])], o["inc"])
                if engname == final_wait_eng:
                    for kk, v in finals:
                        handle.wait_ge(sems[kk], v)

            with nc.Block() as block:
                @block.tensor
                def _(h):
                    emit("pe", h)

                @block.scalar
                def _(h):
                    emit("act", h)

                @block.vector
                def _(h):
                    emit("dve", h)

                @block.gpsimd
                def _(h):
                    emit("pool", h)

                @block.sync
                def _(h):
                    emit("sp", h)
        return dict(eng_count=eng_count, n_ops=len(ops), n_sems=len(sems))


LAST_KB = None
SB_BASE = 16640
SB_END = 229376


class KB:
    def __init__(self, stage):
        self.stage = stage
        self.nc = bass.Bass("TRN2", target_bir_lowering=False)
        self.P = Prog(self.nc)
        self.dram = {}
        self.tens = {}
        self._n = 0
        global LAST_KB
        LAST_KB = self

    def sb(self, name, shape, dt, off):
        self._n += 1
        nbytes = int(np.prod(shape[1:])) * (2 if dt == BF16 else 4)
        assert off % 32 == 0 and off >= SB_BASE and off + nbytes <= SB_END, (name, off, nbytes)
        t = self.nc.alloc_sbuf_tensor_at(f"{name}_{self._n}", list(shape), dt, offset=off)
        self.tens[name] = t
        return t

    def din(self, name, shape, dt=F32):
        t = self.nc.dram_tensor(name, list(shape), dt, kind="ExternalInput").ap()
        self.dram[name] = t
        return t

    def dout(self, name, shape, dt=F32):
        t = self.nc.dram_tensor(name, list(shape), dt, kind="ExternalOutput").ap()
        self.dram[name] = t
        return t

    def dint(self, name, shape, dt=F32):
        return self.nc.dram_tensor(name, list(shape), dt).ap()


def xres(i, n=None):
    if n is None:
        return [("x", i, k) for k in range(4)]
    return [("x", i, n)]


def build_program(stage):
    K = KB(stage)
    nc, P = K.nc, K.P
    xp = K.din("xp", [TP, D])
    xs = K.din("xs", [TS, D])
    g_ffn1 = K.din("g_ffn1", [2, D])
    g_mix = K.din("g_mix", [2, D])
    g_ffn2 = K.din("g_ffn2", [2, D])
    g_final = K.din("g_final", [1, D])
    wf = {}
    if not stage.get("noffn"):
        for nm in ("w_ffn1_gate", "w_ffn1_up", "w_ffn2_gate", "w_ffn2_up"):
            wf[nm] = K.din(nm, [2, D, DFF])
        for nm in ("w_ffn1_down", "w_ffn2_down"):
            wf[nm] = K.din(nm, [2, DFF, D])
    yp = K.dout("yp", [TP, D])
    ys = K.dout("ys", [TS, D])

    o = SB_BASE
    x_tm = K.sb("x", [128, NT, D], F32, o); o += NT * D * 4
    hT = K.sb("hT", [128, 16, T], BF16, o); o += 16 * T * 2
    misc = o
    ident_bf = K.sb("identb", [128, 128], BF16, o); o += 256
    ss = K.sb("ss", [128, 16], F32, o); o += 64
    rstd = K.sb("rstd", [128, 16], F32, o); o += 64
    sg = [K.sb(f"sg{i}", [128, 352], F32, o + i * 1408) for i in range(2)]; o += 2816
    small = K.sb("small", [128, 64], F32, o); o += 256
    cst = K.sb("cst", [128, CW], F32, o); o += CW * 4
    o = (o + 31) // 32 * 32
    cstb = K.sb("cstb", [128, 258], BF16, o); o += 544
    assert o <= misc + 8192, o
    junk = K.sb("junk", [128, D], BF16, misc + 8192)
    o = misc + 12288
    BIG = o
    wgu = []
    for s in range(3):
        wgu.append((K.sb(f"wg{s}", [128, 16, 256], BF16, o), K.sb(f"wu{s}", [128, 16, 256], BF16, o + 8192)))
        o += 16384
    wdn = []
    for s in range(3):
        wdn.append(K.sb(f"wd{s}", [128, 2, D], BF16, o)); o += 8192
    actb = []
    act_off = o
    for s in range(2):
        actb.append(K.sb(f"act{s}", [128, 4, T], BF16, o)); o += 4 * T * 2
    assert o <= SB_END, o
    g_bc = K.sb("gbc", [128, D], F32, act_off)
    h_tm = [K.sb(f"htm{i}", [128, D], BF16, act_off + 8192 + i * 4096) for i in range(2)]
    NORM_OVL = [("act", 0), ("act", 1)]

    psb = [nc.alloc_psum_tensor(f"ps{b}", [128, 512], F32) for b in range(8)]

    def ps_bf(b):
        return psb[b][:].bitcast(BF16)

    P.op("pool", lambda h: h.memset(ident_bf[:], 0.0), writes=["identb"])
    P.op("pool", lambda h: h.affine_select(out=ident_bf[:], in_=ident_bf[:], pattern=[[-1, 128]],
                                           compare_op=ALU.not_equal, fill=1.0, base=0, channel_multiplier=1),
         reads=["identb"], writes=["identb"])

    P.op("pool", lambda h: h.memset(ss[:], 1.0), writes=[("ss", i) for i in range(NT)])
    P.op("pool", lambda h: h.memset(rstd[:], 1.0), writes=["rstd"])
    P.op("pool", lambda h: h.memset(small[:], 1.0), writes=[("small", j) for j in range(8)])
    P.op("pool", lambda h: h.memset(x_tm[:, 8, :], 0.0), writes=xres(8))
    xpv = xp.rearrange("(i p) d -> p i d", p=128)
    for i in range(8):
        P.dma("sp", [lambda h, i=i: h.dma_start(out=x_tm[:, i, :], in_=xpv[:, i, :])], ("xin", i), writes=xres(i))
    P.dma("sp", [lambda h: h.dma_start(out=x_tm[0:TS, 8, :], in_=xs[:, :])], ("xin", 8), writes=xres(8))

    def rmsnorm_hT(g_row, tag):
        P.dma("sp", [lambda h: h.dma_start(out=g_bc[:], in_=g_row.partition_broadcast(128))], "gbc",
              writes=["gbc"] + NORM_OVL)
        for i in range(NT):
            R = rows(i)
            P.op("act", lambda h, i=i, R=R: h.activation(out=junk[:R, :], in_=x_tm[:R, i, :], func=AF.Square,
                                                         accum_out=ss[:R, i:i + 1]),
                 reads=xres(i), writes=["junk", ("ss", i)])
        P.op("dve", lambda h: h.tensor_scalar(out=rstd[:, 0:NT], in0=ss[:, 0:NT], scalar1=1.0 / D, scalar2=EPS,
                                              op0=ALU.mult, op1=ALU.add),
             reads=[("ss", i) for i in range(NT)], writes=["rstd"])
        P.op("act", lambda h: h.activation(out=rstd[:, 0:NT], in_=rstd[:, 0:NT], func=AF.Sqrt), reads=["rstd"], writes=["rstd"])
        P.op("dve", lambda h: h.reciprocal(out=rstd[:, 0:NT], in_=rstd[:, 0:NT]), reads=["rstd"], writes=["rstd"])
        for i in range(NT):
            R = rows(i)
            hb = h_tm[i % 2]
            P.op("dve", lambda h, i=i, R=R, hb=hb: h.scalar_tensor_tensor(
                out=hb[:R, :], in0=x_tm[:R, i, :], scalar=rstd[:R, i:i + 1], in1=g_bc[:R, :], op0=ALU.mult, op1=ALU.mult),
                 reads=xres(i) + ["rstd", "gbc"], writes=[("htm", i % 2)])
            for half in range(2):
                bank = 6 + half
                pv = ps_bf(bank).rearrange("p (q c) -> p q c", c=128)
                for q in range(8):
                    dc = half * 8 + q
                    P.op("pe", lambda h, q=q, dc=dc, R=R, hb=hb, pv=pv: h.transpose(
                        out=pv[:, q, :R], in_=hb[:R, dc * 128:(dc + 1) * 128], identity=ident_bf[:R, :R]),
                         reads=[("htm", i % 2), "identb"], writes=[("ps", bank)], signal=(q == 7))
                eng = "act" if half == 0 else "dve"
                if eng == "act":
                    fn = lambda h, half=half, i=i, R=R, pv=pv: h.copy(out=hT[:, half * 8:(half + 1) * 8, i * 128:i * 128 + R], in_=pv[:, :, :R])
                else:
                    fn = lambda h, half=half, i=i, R=R, pv=pv: h.tensor_copy(out=hT[:, half * 8:(half + 1) * 8, i * 128:i * 128 + R], in_=pv[:, :, :R])
                P.op(eng, fn, reads=[("ps", bank)], writes=[("hT", i)])

    TB = [(0, 352), (352, 352), (704, 352)]

    def tb_tiles(tb):
        c0, n = TB[tb]
        return [("hT", i) for i in range(c0 // 128, min(NT, (c0 + n + 127) // 128))]

    def ffn(layer, wg_d, wu_d, wd_d):
        wgv = wg_d[layer].rearrange("(dc p) f -> p dc f", p=128)
        wuv = wu_d[layer].rearrange("(dc p) f -> p dc f", p=128)
        wdv = wd_d[layer].rearrange("(j p) d -> p j d", p=128)
        NU = DFF // 256

        def load_wgu(u):
            s = u % 3
            P.dma("pool", [lambda h, u=u, s=s: h.dma_start(out=wgu[s][0][:], in_=wgv[:, :, u * 256:(u + 1) * 256]),
                           lambda h, u=u, s=s: h.dma_start(out=wgu[s][1][:], in_=wuv[:, :, u * 256:(u + 1) * 256])],
                  ("wgu", s), writes=[("wgu", s)])

        def load_wd(u):
            s = u % 3
            P.dma("pool", [lambda h, u=u, s=s: h.dma_start(out=wdn[s][:], in_=wdv[:, 2 * u:2 * u + 2, :])],
                  ("wd", s), writes=[("wd", s)])

        cnt = {"gu": 0, "d": 0}

        def GU(u):
            s = u % 3
            g = u // 2
            ab = actb[g % 2]
            for jj in range(2):
                j4 = (u % 2) * 2 + jj
                for tb in range(3):
                    c0, n = TB[tb]
                    k = cnt["gu"]; cnt["gu"] += 1
                    bg, bu = k % 2, 2 + k % 2
                    for dc in range(16):
                        P.op("pe", lambda h, dc=dc, s=s, jj=jj, c0=c0, n=n, bg=bg: h.matmul(
                            psb[bg][:, :n], lhsT=wgu[s][0][:, dc, jj * 128:(jj + 1) * 128], rhs=hT[:, dc, c0:c0 + n],
                            start=(dc == 0), stop=(dc == 15)),
                             reads=[("wgu", s)] + tb_tiles(tb), writes=[("ps", bg)], signal=(dc == 15))
                    for dc in range(16):
                        P.op("pe", lambda h, dc=dc, s=s, jj=jj, c0=c0, n=n, bu=bu: h.matmul(
                            psb[bu][:, :n], lhsT=wgu[s][1][:, dc, jj * 128:(jj + 1) * 128], rhs=hT[:, dc, c0:c0 + n],
                            start=(dc == 0), stop=(dc == 15)),
                             reads=[("wgu", s)] + tb_tiles(tb), writes=[("ps", bu)], signal=(dc == 15))
                    sgb = sg[k % 2]
                    P.op("act", lambda h, n=n, bg=bg, sgb=sgb: h.activation(out=sgb[:, :n], in_=psb[bg][:, :n], func=AF.Silu),
                         reads=[("ps", bg)], writes=[("sg", k % 2)])
                    P.op("dve", lambda h, n=n, bu=bu, sgb=sgb, ab=ab, j4=j4, c0=c0: h.tensor_tensor(
                        out=ab[:, j4, c0:c0 + n], in0=sgb[:, :n], in1=psb[bu][:, :n], op=ALU.mult),
                         reads=[("sg", k % 2), ("ps", bu)], writes=[("act", g % 2), ("htm", 0), ("htm", 1), "gbc"])

        def DN(g):
            ab = actb[g % 2]
            for i in range(NT):
                R = rows(i)
                for n4 in range(4):
                    k = cnt["d"]; cnt["d"] += 1
                    b = 4 + k % 2
                    for j4 in range(4):
                        u = 2 * g + j4 // 2
                        s = u % 3
                        P.op("pe", lambda h, i=i, R=R, n4=n4, j4=j4, s=s, b=b, ab=ab: h.matmul(
                            psb[b][:R, :], lhsT=ab[:, j4, i * 128:i * 128 + R], rhs=wdn[s][:, j4 % 2, n4 * 512:(n4 + 1) * 512],
                            start=(j4 == 0), stop=(j4 == 3)),
                             reads=[("act", g % 2), ("wd", s)], writes=[("ps", b)], signal=(j4 == 3))
                    P.op("dve", lambda h, i=i, R=R, n4=n4, b=b: h.scalar_tensor_tensor(
                        out=x_tm[:R, i, n4 * 512:(n4 + 1) * 512], in0=psb[b][:R, :], scalar=0.5,
                        in1=x_tm[:R, i, n4 * 512:(n4 + 1) * 512], op0=ALU.mult, op1=ALU.add),
                         reads=[("ps", b)] + xres(i, n4), writes=xres(i, n4))

        for u in range(3):
            load_wgu(u)
        for u in range(3):
            load_wd(u)
        for u in range(NU):
            GU(u)
            if u + 3 < NU:
                load_wgu(u + 3)
            if u >= 2 and u % 2 == 0:
                g = u // 2 - 1
                DN(g)
                for uu in (2 * g + 3, 2 * g + 4):
                    if uu < NU:
                        load_wd(uu)
        DN(NU // 2 - 1)


    def final_out():
        P.dma("sp", [lambda h: h.dma_start(out=g_bc[:], in_=g_final[0:1, :].partition_broadcast(128))], "gbc",
              writes=["gbc"] + NORM_OVL)
        for i in range(NT):
            R = rows(i)
            P.op("act", lambda h, i=i, R=R: h.activation(out=junk[:R, :], in_=x_tm[:R, i, :], func=AF.Square,
                                                         accum_out=ss[:R, i:i + 1]),
                 reads=xres(i), writes=["junk", ("ss", i)])
        P.op("dve", lambda h: h.tensor_scalar(out=rstd[:, 0:NT], in0=ss[:, 0:NT], scalar1=1.0 / D, scalar2=EPS,
                                              op0=ALU.mult, op1=ALU.add),
             reads=[("ss", i) for i in range(NT)], writes=["rstd"])
        P.op("act", lambda h: h.activation(out=rstd[:, 0:NT], in_=rstd[:, 0:NT], func=AF.Sqrt), reads=["rstd"], writes=["rstd"])
        P.op("dve", lambda h: h.reciprocal(out=rstd[:, 0:NT], in_=rstd[:, 0:NT]), reads=["rstd"], writes=["rstd"])
        ypv = yp.rearrange("(i p) d -> p i d", p=128)
        for i in range(NT):
            R = rows(i)
            P.op("dve", lambda h, i=i, R=R: h.scalar_tensor_tensor(
                out=x_tm[:R, i, :], in0=x_tm[:R, i, :], scalar=rstd[:R, i:i + 1], in1=g_bc[:R, :], op0=ALU.mult, op1=ALU.mult),
                 reads=xres(i) + ["rstd", "gbc"], writes=xres(i))
            if i < 8:
                P.dma("sp", [lambda h, i=i: h.dma_start(out=ypv[:, i, :], in_=x_tm[:, i, :])], ("yout", i), reads=xres(i))
            else:
                P.dma("sp", [lambda h: h.dma_start(out=ys[:, :], in_=x_tm[0:TS, 8, :])], ("yout", i), reads=xres(i))

    class Arena:
        def __init__(self, lo, hi, tag):
            self.lo, self.hi, self.o, self.tag, self.names = lo, hi, lo, tag, []

        def alloc(self, name, shape, dt):
            nb = int(np.prod(shape[1:])) * (2 if dt == BF16 else 4)
            nb = (nb + 31) // 32 * 32
            assert self.o + nb <= self.hi, (self.tag, name, self.o + nb - self.hi)
            t = K.sb(name, shape, dt, self.o)
            self.o += nb
            return t

        def reset(self, to=None):
            self.o = self.lo if to is None else to

    AA = Arena(SB_BASE, SB_BASE + NT * D * 4, "A")
    AB_ = Arena(BIG, SB_END, "B")
    xsp = K.dint("xsp", [128, NT, D])

    def fence():
        P.fence()

    def spill_x():
        fence()
        P.dma("sp", [lambda h: h.dma_start(out=xsp[:, :, :], in_=x_tm[:, :, :])], "xsp", reads=[r for i in range(NT) for r in xres(i)], writes=["xsp"])
        fence()

    def reload_x():
        fence()
        P.dma("sp", [lambda h: h.dma_start(out=x_tm[:, :, :], in_=xsp[:, :, :])], "xrl", reads=["xsp"], writes=[r for i in range(NT) for r in xres(i)])
        fence()

    def tcols(i):
        return i * 128, rows(i)

    cnt = {"n": 0}

    def uid():
        cnt["n"] += 1
        return cnt["n"]

    def transposes_to(src_tm, R, nblk, bank, dst_fn, eng="dve", src_res=(), dst_res=(), blk=128, mul_in1=None):
        pv = ps_bf(bank)[:, 0:nblk * 128].rearrange("p (q c) -> p q c", c=128)
        for q in range(nblk):
            P.op("pe", lambda h, q=q: h.transpose(out=pv[:blk, q, :R], in_=src_tm[:R, q * blk:(q + 1) * blk], identity=ident_bf[:R, :R]),
                 reads=list(src_res) + ["identb"], writes=[("ps", bank)], signal=(q == nblk - 1))
        dst = dst_fn()
        if mul_in1 is not None:
            P.op("dve", lambda h: h.tensor_tensor(out=dst, in0=pv[:blk, 0:nblk, :R], in1=mul_in1, op=ALU.mult), reads=[("ps", bank)], writes=list(dst_res))
        elif eng == "act":
            P.op("act", lambda h: h.copy(out=dst, in_=pv[:blk, 0:nblk, :R]), reads=[("ps", bank)], writes=list(dst_res))
        else:
            P.op("dve", lambda h: h.tensor_copy(out=dst, in_=pv[:blk, 0:nblk, :R]), reads=[("ps", bank)], writes=list(dst_res))

    def norm_rows(R, bank, n, gb, out_ap, k, out_res, scale_cols=None):
        c = 2 * (k % 8)
        sr = ("small", k % 8)
        P.op("act", lambda h: h.activation(out=junk[:R, :n], in_=psb[bank][:R, :n], func=AF.Square, accum_out=small[:R, c:c + 1]),
             reads=[("ps", bank)], writes=["junk", sr])
        P.op("dve", lambda h: h.tensor_scalar(out=small[:R, c + 1:c + 2], in0=small[:R, c:c + 1], scalar1=1.0 / n, scalar2=EPS, op0=ALU.mult, op1=ALU.add), reads=[sr], writes=[sr])
        P.op("act", lambda h: h.activation(out=small[:R, c + 1:c + 2], in_=small[:R, c + 1:c + 2], func=AF.Sqrt), reads=[sr], writes=[sr])
        P.op("dve", lambda h: h.reciprocal(out=small[:R, c + 1:c + 2], in_=small[:R, c + 1:c + 2]), reads=[sr], writes=[sr])
        P.op("dve", lambda h: h.scalar_tensor_tensor(out=out_ap, in0=psb[bank][:R, :n], scalar=small[:R, c + 1:c + 2], in1=gb[:R, :n], op0=ALU.mult, op1=ALU.mult),
             reads=[("ps", bank), sr, "gsm"], writes=list(out_res))

    def allgather(src, dst, key, reads, writes):
        P.dma("pool", [lambda h: h.collective_compute("AllGather", ALU.bypass, replica_groups=[list(range(NCORES))],
                                                      ins=[src.opt()], outs=[dst.opt()])], key, reads=reads, writes=writes, inc=1)

    def ab_mixer():
        w_in = K.din("w_ab_in", [1, D, 4176])
        w_gate = K.din("w_gla_gate_up", [1, 16, 512])
        b_gate = K.din("b_gla_gate", [1, 512])
        g_gla = K.din("g_gla_norm", [1, 256])
        g_qn = K.din("g_mla_q_norm", [1, 512])
        w_uq = K.din("w_mla_uq", [1, 512, 1536])
        g_kvn = K.din("g_mla_kv_norm", [1, 512])
        w_uk = K.din("w_mla_uk", [1, 512, 1024])
        w_uv = K.din("w_mla_uv", [1, 512, 1024])
        w_out = K.din("w_ab_out", [1, D, D])
        c_ckv = K.din("c_ckv", [1024, 512])
        c_kpe = K.din("c_kpe", [1024, 64])
        st_gla = K.din("st_gla", [4, 128, 256])
        o_pckv = K.dout("p_ckv", [TP, 512]); o_sckv = K.dout("s_ckv", [TS, 512])
        o_pkpe = K.dout("p_kpe", [TP, 64]); o_skpe = K.dout("s_kpe", [TS, 64])
        o_pgla = K.dout("p_gla", [4, 128, 256]); o_sgla = K.dout("s_gla", [4, 128, 256])
        bounce1 = K.dint("bounce1", [2112, 1024], BF16)
        gath1 = K.dint("gath1", [NCORES * 2112, 1024], BF16)
        bounce2 = K.dint("bounce2", [128, 1032])
        gath2 = K.dint("gath2", [NCORES * 128, 1032])
        winv = w_in[0].rearrange("(dc p) f -> p dc f", p=128)

        rmsnorm_hT(g_mix[0:1, :], "mix0")
        spill_x()
        AA.reset(); AB_.reset()
        wA = [AA.alloc(f"wA{s}", [128, 16, 512], BF16) for s in range(2)]
        wcnt = {"n": 0}

        def load_wA(col0, ncols):
            s = wcnt["n"] % 2; wcnt["n"] += 1
            P.dma("pool", [lambda h: h.dma_start(out=wA[s][:, :, :ncols], in_=winv[:, :, col0:col0 + ncols])], ("wA", s), writes=[("wA", s)])
            return s

        def proj_tm(s, ncols, consumer, banks=(0, 1)):
            for i in range(NT):
                c0, R = tcols(i)
                bank = banks[i % len(banks)]
                for dc in range(16):
                    P.op("pe", lambda h, dc=dc, c0=c0, R=R, bank=bank: h.matmul(psb[bank][:R, :ncols], lhsT=hT[:, dc, c0:c0 + R], rhs=wA[s][:, dc, :ncols], start=(dc == 0), stop=(dc == 15)),
                         reads=[("hT", i), ("wA", s)], writes=[("ps", bank)], signal=(dc == 15))
                consumer(i, R, bank)

        def proj_fm(s, mcol0, M, consumer, banks=(0, 1)):
            for tb in range(3):
                c0, n = TB[tb]
                bank = banks[tb % len(banks)]
                for dc in range(16):
                    P.op("pe", lambda h, dc=dc, c0=c0, n=n, bank=bank: h.matmul(psb[bank][:M, :n], lhsT=wA[s][:, dc, mcol0:mcol0 + M], rhs=hT[:, dc, c0:c0 + n], start=(dc == 0), stop=(dc == 15)),
                         reads=tb_tiles(tb) + [("wA", s)], writes=[("ps", bank)], signal=(dc == 15))
                consumer(tb, c0, n, bank)

        cqnT = AB_.alloc("cqnT", [128, 4, T], BF16)
        kpeT = AB_.alloc("kpeT", [128, 2080], BF16)
        markB1 = AB_.o
        gq_bc = AB_.alloc("gqbc", [128, 512], F32)
        gkv_bc = AB_.alloc("gkvbc", [128, 512], F32)
        ckvnT = AB_.alloc("ckvnT", [128, 4, 2080], BF16)
        markB = AB_.o
        wuk = AA.alloc("wuk", [128, 4, 1024], BF16)
        wuv = AA.alloc("wuv", [128, 4, 1024], BF16)
        tmb = [AB_.alloc(f"tmb{j}", [128, 512], BF16) for j in range(2)]
        tmf = [AB_.alloc(f"tmf{j}", [128, 512], F32) for j in range(2)]
        rp = [AB_.alloc(f"rp{j}", [128, 64], F32) for j in range(4)]
        ckc = AA.alloc("ckc", [128, 8, 512], BF16)
        kpc = AA.alloc("kpc", [128, 8, 64], BF16)
        kst = [AA.alloc(f"kst{j}", [128, 1024], BF16) for j in range(2)]
        vst = [AB_.alloc(f"vst{j}", [128, 8, 128], BF16) for j in range(2)]
        P.dma("sp", [lambda h: h.dma_start(out=gq_bc[:], in_=g_qn[0:1, :].partition_broadcast(128)),
                     lambda h: h.dma_start(out=gkv_bc[:], in_=g_kvn[0:1, :].partition_broadcast(128))], "gsm", writes=["gsm"])
        P.dma("pool", [lambda h: h.dma_start(out=wuk[:], in_=w_uk[0].rearrange("(c p) f -> p c f", p=128)),
                       lambda h: h.dma_start(out=wuv[:], in_=w_uv[0].rearrange("(c p) f -> p c f", p=128)),
                       lambda h: h.dma_start(out=ckc[:], in_=c_ckv.rearrange("(j p) f -> p j f", p=128)),
                       lambda h: h.dma_start(out=kpc[:], in_=c_kpe.rearrange("(j p) f -> p j f", p=128))], "wukv", writes=["wukv"])

        s = load_wA(3088, 512)

        def cons_cq(i, R, bank):
            k = uid()
            b = tmb[k % 2]
            norm_rows(R, bank, 512, gq_bc, b[:R, :], k, [("tmb", k % 2)])
            c0 = i * 128
            transposes_to(b, R, 4, 6 + k % 2, lambda: cqnT[:, 0:4, c0:c0 + R], eng="act", src_res=[("tmb", k % 2)], dst_res=[("cqnT", i)])
        proj_tm(s, 512, cons_cq)

        s = load_wA(3600, 512)

        def cons_ckv(i, R, bank):
            k = uid()
            f, b = tmf[k % 2], tmb[k % 2]
            norm_rows(R, bank, 512, gkv_bc, f[:R, :], k, [("tmf", k % 2)])
            if i < 8:
                P.dma("sp", [lambda h: h.dma_start(out=o_pckv[i * 128:(i + 1) * 128, :], in_=f[:, :])], ("o_ckv", k % 2), reads=[("tmf", k % 2)])
            else:
                P.dma("sp", [lambda h: h.dma_start(out=o_sckv[:, :], in_=f[:TS, :])], ("o_ckv", k % 2), reads=[("tmf", k % 2)])
            P.op("act", lambda h: h.copy(out=b[:R, :], in_=f[:R, :]), reads=[("tmf", k % 2)], writes=[("tmb", k % 2)])
            c0 = i * 128
            transposes_to(b, R, 4, 6 + k % 2, lambda: ckvnT[:, 0:4, c0:c0 + R], src_res=[("tmb", k % 2)], dst_res=["ckvnT"])
        proj_tm(s, 512, cons_ckv)

        s = load_wA(4112, 64)

        def rope(R, src1, src2, cosb, sinb, o1, o2, t1, t2, rd, wr, tr):
            P.op("dve", lambda h: h.tensor_tensor(out=t1, in0=src1, in1=cosb, op=ALU.mult), reads=rd, writes=[tr])
            P.op("dve", lambda h: h.tensor_tensor(out=t2, in0=src2, in1=sinb, op=ALU.mult), reads=rd, writes=[tr])
            P.op("dve", lambda h: h.tensor_tensor(out=o1, in0=t1, in1=t2, op=ALU.subtract), reads=[tr], writes=wr)
            P.op("dve", lambda h: h.tensor_tensor(out=t1, in0=src2, in1=cosb, op=ALU.mult), reads=rd, writes=[tr])
            P.op("dve", lambda h: h.tensor_tensor(out=t2, in0=src1, in1=sinb, op=ALU.mult), reads=rd, writes=[tr])
            P.op("dve", lambda h: h.tensor_tensor(out=o2, in0=t1, in1=t2, op=ALU.add), reads=[tr], writes=wr)

        def cons_kpe(i, R, bank):
            k = uid()
            f = rp[k % 2]
            tt = rp[2 + k % 2]
            cosb = cst[:R, C_COS + 32 * i:C_COS + 32 * i + 32]
            sinb = cst[:R, C_SIN + 32 * i:C_SIN + 32 * i + 32]
            rope(R, psb[bank][:R, 0:32], psb[bank][:R, 32:64], cosb, sinb, f[:R, 0:32], f[:R, 32:64], tt[:R, 0:32], tt[:R, 32:64],
                 [("ps", bank), "cst"], [("rp", k % 2)], ("rpt", k % 2))
            if i < 8:
                P.dma("sp", [lambda h: h.dma_start(out=o_pkpe[i * 128:(i + 1) * 128, :], in_=f[:, :])], ("o_kpe", k % 2), reads=[("rp", k % 2)])
            else:
                P.dma("sp", [lambda h: h.dma_start(out=o_skpe[:, :], in_=f[:TS, :])], ("o_kpe", k % 2), reads=[("rp", k % 2)])
            b = tmb[k % 2]
            P.op("act", lambda h: h.copy(out=b[:R, 0:64], in_=f[:R, :]), reads=[("rp", k % 2)], writes=[("tmb", k % 2)])
            c0 = i * 128
            transposes_to(b, R, 1, 6 + k % 2, lambda: kpeT[0:64, c0:c0 + R].unsqueeze(1), src_res=[("tmb", k % 2)], dst_res=["kpeT"], blk=64)
        proj_tm(s, 64, cons_kpe)

        for j in range(8):
            k = uid()
            transposes_to(ckc[:, j, :], 128, 4, 6 + k % 2, lambda j=j: ckvnT[:, 0:4, T + j * 128:T + (j + 1) * 128], src_res=["wukv"], dst_res=["ckvnT"])
            k = uid()
            transposes_to(kpc[:, j, :], 128, 1, 6 + k % 2, lambda j=j: kpeT[0:64, T + j * 128:T + (j + 1) * 128].unsqueeze(1), eng="act", src_res=["wukv"], dst_res=["kpeT"], blk=64)

        bv = bounce1[1088:2112, :].rearrange("(h a) (b d) -> h (a b) d", h=8, b=8, d=128)

        def kv_expand(CB, vtiles, kT_s, v_s, vst):
            for hh in range(8):
                ks = kst[hh % 2]
                for cb, (c0, n) in enumerate(CB):
                    k = uid(); bank = k % 2
                    for c in range(4):
                        P.op("pe", lambda h, c=c, c0=c0, n=n, bank=bank, hh=hh: h.matmul(psb[bank][:, :n], lhsT=wuk[:, c, hh * 128:(hh + 1) * 128], rhs=ckvnT[:, c, c0:c0 + n], start=(c == 0), stop=(c == 3)),
                             reads=["wukv", "ckvnT"], writes=[("ps", bank)], signal=(c == 3))
                    if c0 < 1024:
                        dst, wr = ks[:, c0:c0 + n], [("kst", hh % 2)]
                    else:
                        dst, wr = kT_s[:, hh, c0 - 1024:c0 - 1024 + n], ["kTs"]
                    if k % 2:
                        P.op("act", lambda h, dst=dst, n=n, bank=bank: h.copy(out=dst, in_=psb[bank][:, :n]), reads=[("ps", bank)], writes=wr)
                    else:
                        P.op("dve", lambda h, dst=dst, n=n, bank=bank: h.tensor_copy(out=dst, in_=psb[bank][:, :n]), reads=[("ps", bank)], writes=wr)
                if CB[0][0] < 1024:
                    P.dma("sp", [lambda h, hh=hh, ks=ks: h.dma_start(out=bounce1[hh * 128:(hh + 1) * 128, :], in_=ks[:, :])], ("bk", hh % 2), reads=[("kst", hh % 2)], writes=["bounce1"])
            for j in vtiles:
                if j < 8:
                    c0, R = j * 128, 128
                elif j == 8:
                    c0, R = 1024, TS
                else:
                    c0, R = T + (j - 9) * 128, 128
                for half in range(2):
                    k = uid(); bank = k % 2
                    for c in range(4):
                        P.op("pe", lambda h, c=c, c0=c0, R=R, bank=bank, half=half: h.matmul(psb[bank][:R, :], lhsT=ckvnT[:, c, c0:c0 + R], rhs=wuv[:, c, half * 512:(half + 1) * 512], start=(c == 0), stop=(c == 3)),
                             reads=["wukv", "ckvnT"], writes=[("ps", bank)], signal=(c == 3))
                    if j < 8:
                        dst, wr = vst[j % 2][:, half * 4:(half + 1) * 4, :], [("vst", j % 2)]
                    else:
                        dst, wr = v_s[:R, j - 8, half * 4:(half + 1) * 4, 0:128], ["vs"]
                    src = psb[bank][:R, :].rearrange("p (h d) -> p h d", d=128)
                    if k % 2:
                        P.op("act", lambda h, dst=dst, src=src: h.copy(out=dst, in_=src), reads=[("ps", bank)], writes=wr)
                    else:
                        P.op("dve", lambda h, dst=dst, src=src: h.tensor_copy(out=dst, in_=src), reads=[("ps", bank)], writes=wr)
                if j < 8:
                    P.dma("sp", [lambda h, j=j: h.dma_start(out=bv[:, j * 128:(j + 1) * 128, :].rearrange("h p d -> p h d"), in_=vst[j % 2][:, :, :])], ("bv", j % 2), reads=[("vst", j % 2)], writes=["bounce1"])

        kv_expand([(0, 512), (512, 512)], range(8), None, None, vst)
        P.dma("sp", [lambda h: h.dma_start(out=bounce1[1024:1088, :], in_=kpeT[0:64, 0:1024])], "bkpe", reads=["kpeT"], writes=["bounce1"])
        allgather(bounce1, gath1, "ag1", ["bounce1"], ["gath1"])
        if stage.get("stop1"):
            reload_x(); return

        fence()
        AB_.reset(markB)
        AA.reset(); wA2 = [AA.alloc(f"wA{s}x", [128, 16, 512], BF16) for s in range(2)]
        v_tm = AA.alloc("vtm", [128, NT, 1024], BF16)
        sogT = AA.alloc("sogT", [128, 8, T], BF16)
        qT = AB_.alloc("qT", [128, 4, T], BF16)
        kT = AB_.alloc("kT", [128, 4, T], BF16)
        kh = AB_.alloc("kh", [128, NT, 512], BF16)
        ext = [AB_.alloc(f"ext{j}", [128, 512], F32) for j in range(3)]
        Dall = AB_.alloc("Dall", [128, NT, 8], F32)
        S2 = AB_.alloc("S2", [128, 4, 256], F32)
        S_bf = AB_.alloc("Sbf", [128, 4, 256], BF16)
        grT = AB_.alloc("grT", [128, T], BF16)
        wgt = AB_.alloc("wgt", [128, 512], BF16)
        bg_bc = AB_.alloc("bgbc", [128, 512], F32)
        gg_bc = AB_.alloc("ggbc", [128, 256], F32)
        spb = [AB_.alloc(f"sp{j}", [128, 512], F32) for j in range(2)]
        sphl = [AB_.alloc(f"sphl{j}", [128, 512], BF16) for j in range(2)]
        ATb = [AB_.alloc(f"AT{j}", [128, 4, 128], BF16) for j in range(2)]
        onb = [AB_.alloc(f"on{j}", [128, 1024], BF16) for j in range(2)]
        bnc = AB_.alloc("bnc", [128, 8], F32)
        ff = AB_.alloc("ff", [128, 8], F32)
        Sin = S2
        P.dma("pool", [lambda h: h.dma_start(out=wgt[0:16, :], in_=w_gate[0])], "wgt", writes=["wgt"])
        P.dma("sp", [lambda h: h.dma_start(out=bg_bc[:], in_=b_gate[0:1, :].partition_broadcast(128)),
                     lambda h: h.dma_start(out=gg_bc[:], in_=g_gla[0:1, :].partition_broadcast(128))], "gsm2", writes=["gsm2"])
        s = load_wA(2048, 16)
        proj_fm(s, 0, 16, lambda tb, c0, n, bank: P.op("act", lambda h: h.copy(out=grT[0:16, c0:c0 + n], in_=psb[bank][0:16, :n]), reads=[("ps", bank)], writes=["grT"]))
        wcnt["n"] = 0
        s_q = load_wA(0, 512)
        s_k = load_wA(512, 512)
        for i in range(NT):
            c0, R = tcols(i)
            k = uid(); bank = k % 2
            spt = spb[k % 2]
            P.op("pe", lambda h, c0=c0, R=R, bank=bank: h.matmul(psb[bank][:R, :], lhsT=grT[0:16, c0:c0 + R], rhs=wgt[0:16, :], start=True, stop=True), reads=["grT", "wgt"], writes=[("ps", bank)])
            P.op("dve", lambda h, R=R, bank=bank, spt=spt: h.tensor_tensor(out=spt[:R, :], in0=psb[bank][:R, :], in1=bg_bc[:R, :], op=ALU.add), reads=[("ps", bank), "gsm2"], writes=[("sp", k % 2)])
            P.op("act", lambda h, R=R, spt=spt: h.activation(out=spt[:R, :], in_=spt[:R, :], func=AF.Exp, scale=-1.0), reads=[("sp", k % 2)], writes=[("sp", k % 2)])
            P.op("act", lambda h, R=R, spt=spt: h.activation(out=spt[:R, :], in_=spt[:R, :], func=AF.Ln, bias=cst[:R, C_ONE:C_ONE + 1], scale=1.0), reads=[("sp", k % 2), "cst"], writes=[("sp", k % 2)])
            b1, b2, b3 = 2, 3, 4
            sph, spl = sphl[0], sphl[1]
            P.op("act", lambda h, R=R, spt=spt: h.copy(out=sph[:R, :], in_=spt[:R, :]), reads=[("sp", k % 2)], writes=["sph"])
            P.op("dve", lambda h, R=R, spt=spt: h.tensor_tensor(out=spl[:R, :], in0=spt[:R, :], in1=sph[:R, :], op=ALU.subtract), reads=[("sp", k % 2), "sph"], writes=["spl"])
            for (bb, c0t) in ((b1, 0), (b2, 128)):
                P.op("pe", lambda h, R=R, bb=bb, c0t=c0t: h.matmul(psb[bb][:R, :], lhsT=cstb[:R, c0t:c0t + R], rhs=sph[:R, :], start=True, stop=False), reads=["sph", "cstb"], writes=[("ps", bb)], signal=False)
                P.op("pe", lambda h, R=R, bb=bb, c0t=c0t: h.matmul(psb[bb][:R, :], lhsT=cstb[:R, c0t:c0t + R], rhs=spl[:R, :], start=False, stop=True), reads=["spl", "cstb"], writes=[("ps", bb)])
            for hh in range(4):
                P.op("pe", lambda h, R=R, hh=hh: h.matmul(psb[b3][:, 2 * hh:2 * hh + 2], lhsT=sph[:R, hh * 128:(hh + 1) * 128], rhs=cstb[:R, 256:258], start=True, stop=False),
                     reads=["sph", "cstb"], writes=[("ps", b3)], signal=False)
                P.op("pe", lambda h, R=R, hh=hh: h.matmul(psb[b3][:, 2 * hh:2 * hh + 2], lhsT=spl[:R, hh * 128:(hh + 1) * 128], rhs=cstb[:R, 256:258], start=False, stop=True),
                     reads=["spl", "cstb"], writes=[("ps", b3)], signal=(hh == 3))
            P.op("act", lambda h, R=R: h.activation(out=ext[0][:R, :], in_=psb[b1][:R, :], func=AF.Exp), reads=[("ps", b1)], writes=["ext0"])
            P.op("act", lambda h, R=R: h.activation(out=ext[1][:R, :], in_=psb[b1][:R, :], func=AF.Exp, scale=-1.0), reads=[("ps", b1)], writes=["ext1"])
            P.op("act", lambda h, R=R: h.activation(out=ext[2][:R, :], in_=psb[b2][:R, :], func=AF.Exp), reads=[("ps", b2)], writes=["ext2"])
            P.op("act", lambda h, i=i: h.activation(out=Dall[:, i, :], in_=psb[b3][:, 0:8], func=AF.Exp), reads=[("ps", b3)], writes=["Dall"])
            bq = 5
            for dc in range(16):
                P.op("pe", lambda h, dc=dc, c0=c0, R=R: h.matmul(psb[bq][:R, :], lhsT=hT[:, dc, c0:c0 + R], rhs=wA[s_q][:, dc, :], start=(dc == 0), stop=(dc == 15)),
                     reads=[("hT", i), ("wA", s_q)], writes=[("ps", bq)], signal=(dc == 15))
            b = onb[k % 2]
            P.op("dve", lambda h, R=R, b=b: h.scalar_tensor_tensor(out=b[:R, 0:512], in0=psb[bq][:R, :], scalar=128.0 ** -0.5, in1=ext[0][:R, :], op0=ALU.mult, op1=ALU.mult), reads=[("ps", bq), "ext0"], writes=[("on", k % 2)])
            transposes_to(b, R, 4, 6, lambda c0=c0, R=R: qT[:, 0:4, c0:c0 + R], eng="act", src_res=[("on", k % 2)], dst_res=["qT"])
            for dc in range(16):
                P.op("pe", lambda h, dc=dc, c0=c0, R=R: h.matmul(psb[bq][:R, :], lhsT=hT[:, dc, c0:c0 + R], rhs=wA[s_k][:, dc, :], start=(dc == 0), stop=(dc == 15)),
                     reads=[("hT", i), ("wA", s_k)], writes=[("ps", bq)], signal=(dc == 15))
            P.op("dve", lambda h, R=R, b=b: h.tensor_tensor(out=b[:R, 512:1024], in0=psb[bq][:R, :], in1=ext[1][:R, :], op=ALU.mult), reads=[("ps", bq), "ext1"], writes=[("on", k % 2)])
            P.op("dve", lambda h, R=R, i=i: h.tensor_tensor(out=kh[:R, i, :], in0=psb[bq][:R, :], in1=ext[2][:R, :], op=ALU.mult), reads=[("ps", bq), "ext2"], writes=["kh"])
            transposes_to(b[:, 512:1024], R, 4, 7, lambda c0=c0, R=R: kT[:, 0:4, c0:c0 + R], eng="act", src_res=[("on", k % 2)], dst_res=["kT"])
        for g2 in range(2):
            s = load_wA(1024 + 512 * g2, 512)
            proj_tm(s, 512, lambda i, R, bank, g2=g2: P.op("act", lambda h: h.copy(out=v_tm[:R, i, g2 * 512:(g2 + 1) * 512], in_=psb[bank][:R, :]), reads=[("ps", bank)], writes=["vtm"]))
        for g2 in range(2):
            s = load_wA(2064 + 512 * g2, 512)
            for mb in range(4):
                proj_fm(s, mb * 128, 128, lambda tb, c0, n, bank, blk=g2 * 4 + mb: P.op("act", lambda h: h.activation(out=sogT[:, blk, c0:c0 + n], in_=psb[bank][:, :n], func=AF.Silu), reads=[("ps", bank)], writes=["sogT"]))

        def state_update(i, R, St):
            for hh in range(4):
                bank = 2 + hh // 2
                P.op("pe", lambda h, hh=hh, bank=bank: h.matmul(psb[bank][:, (hh % 2) * 256:(hh % 2 + 1) * 256], lhsT=kh[:R, i, hh * 128:(hh + 1) * 128], rhs=v_tm[:R, i, hh * 256:(hh + 1) * 256], start=True, stop=True),
                     reads=["kh", "vtm"], writes=[("ps", bank)], signal=(hh % 2 == 1))
            for hh in range(4):
                bank = 2 + hh // 2
                P.op("dve", lambda h, hh=hh, bank=bank: h.scalar_tensor_tensor(out=St[:, hh, :], in0=St[:, hh, :], scalar=Dall[:, i, 2 * hh:2 * hh + 1], in1=psb[bank][:, (hh % 2) * 256:(hh % 2 + 1) * 256], op0=ALU.mult, op1=ALU.add),
                     reads=[("ps", bank), "Dall", "S2"], writes=["S2"])

        P.op("pool", lambda h: h.memset(S2[:], 0.0), writes=["S2"])
        for i in range(8):
            state_update(i, 128, S2)
        P.op("dve", lambda h: h.tensor_copy(out=bnc[:], in_=Dall[:, 0, :]), reads=["Dall"], writes=["bnc"])
        for i in range(1, 8):
            P.op("dve", lambda h, i=i: h.tensor_tensor(out=bnc[:], in0=bnc[:], in1=Dall[:, i, :], op=ALU.mult), reads=["Dall", "bnc"], writes=["bnc"])
        P.dma("sp", [lambda h: h.dma_start(out=bounce2[:, 0:1024], in_=S2[:].rearrange("p h v -> p (h v)")),
                     lambda h: h.dma_start(out=bounce2[:, 1024:1032], in_=bnc[:])], "b2", reads=["S2", "bnc"], writes=["bounce2"])
        allgather(bounce2, gath2, "ag2", ["bounce2"], ["gath2"])
        P.op("pool", lambda h: h.memset(S2[:], 0.0), reads=["bounce2"], writes=["S2"])
        AA.reset(); G = AA.alloc("G", [128, 7, 1032], F32)
        P.dma("sp", [lambda h: h.dma_start(out=G[:], in_=gath2[0:7 * 128, :].rearrange("(r p) c -> p r c", p=128))], "g2l", reads=["gath2"], writes=["G", ("wA", 0), ("wA", 1)])
        for r in range(7):
            m = cst[:, C_CMASK + r:C_CMASK + r + 1]
            P.op("dve", lambda h, r=r, m=m: h.tensor_scalar(out=ff[:], in0=G[:, r, 1024:1032], scalar1=-1.0, scalar2=m, op0=ALU.add, op1=ALU.mult), reads=["G", "cst"], writes=["ff"])
            P.op("dve", lambda h: h.tensor_scalar_add(out=ff[:], in0=ff[:], scalar1=1.0), reads=["ff"], writes=["ff"])
            for hh in range(4):
                P.op("dve", lambda h, hh=hh: h.tensor_scalar_mul(out=Sin[:, hh, :], in0=Sin[:, hh, :], scalar1=ff[:, 2 * hh:2 * hh + 1]), reads=["ff", "S2"], writes=["S2"])
                P.op("dve", lambda h, hh=hh, r=r, m=m: h.scalar_tensor_tensor(out=Sin[:, hh, :], in0=G[:, r, hh * 256:(hh + 1) * 256], scalar=m, in1=Sin[:, hh, :], op0=ALU.mult, op1=ALU.add), reads=["G", "S2"], writes=["S2"])

        mergedT = hT
        for i in range(NT):
            c0, R = tcols(i)
            if i == 8:
                P.dma("sp", [lambda h: h.dma_start(out=o_pgla.rearrange("h k v -> k h v"), in_=S2[:])], "opgla", reads=["S2"])
                P.dma("sp", [lambda h: h.dma_start(out=S2[:], in_=st_gla.rearrange("h k v -> k h v"))], "stgla", writes=["S2"])
            k = uid()
            P.op("act", lambda h: h.copy(out=S_bf[:], in_=S2[:]), reads=["S2"], writes=["Sbf"])
            bA = 4 + k % 2
            for hh in range(4):
                P.op("pe", lambda h, hh=hh, c0=c0, R=R, bA=bA: h.matmul(psb[bA][:R, hh * 128:hh * 128 + R], lhsT=kT[:, hh, c0:c0 + R], rhs=qT[:, hh, c0:c0 + R], start=True, stop=True),
                     reads=["kT", "qT"], writes=[("ps", bA)], signal=(hh == 3))
            AT = ATb[k % 2]
            for hh in range(4):
                P.op("dve", lambda h, hh=hh, R=R, bA=bA, AT=AT: h.scalar_tensor_tensor(out=AT[:R, hh, :R], in0=psb[bA][:R, hh * 128:hh * 128 + R], scalar=-16.0, in1=cst[:R, C_TRII:C_TRII + R], op0=ALU.mult, op1=ALU.mult),
                     reads=[("ps", bA), "cst"], writes=[("AT", k % 2)])
            for hh in range(4):
                bank = hh // 2
                o_ap = psb[bank][:R, (hh % 2) * 256:(hh % 2 + 1) * 256]
                P.op("pe", lambda h, hh=hh, R=R, o_ap=o_ap, AT=AT: h.matmul(o_ap, lhsT=AT[:R, hh, :R], rhs=v_tm[:R, i, hh * 256:(hh + 1) * 256], start=True, stop=False),
                     reads=[("AT", k % 2), "vtm"], writes=[("ps", bank)], signal=False)
                P.op("pe", lambda h, hh=hh, R=R, c0=c0, o_ap=o_ap: h.matmul(o_ap, lhsT=qT[:, hh, c0:c0 + R], rhs=S_bf[:, hh, :], start=False, stop=True),
                     reads=["qT", "Sbf"], writes=[("ps", bank)], signal=(hh % 2 == 1))
            on = onb[k % 2]
            for hh in range(4):
                bank = hh // 2
                kk = uid(); c = 2 * (kk % 8); sr = ("small", kk % 8)
                o_ap = psb[bank][:R, (hh % 2) * 256:(hh % 2 + 1) * 256]
                P.op("act", lambda h, R=R, o_ap=o_ap, c=c: h.activation(out=junk[:R, :256], in_=o_ap, func=AF.Square, accum_out=small[:R, c:c + 1]), reads=[("ps", bank)], writes=["junk", sr])
                P.op("dve", lambda h, R=R, c=c: h.tensor_scalar(out=small[:R, c + 1:c + 2], in0=small[:R, c:c + 1], scalar1=1.0 / 256, scalar2=EPS, op0=ALU.mult, op1=ALU.add), reads=[sr], writes=[sr])
                P.op("act", lambda h, R=R, c=c: h.activation(out=small[:R, c + 1:c + 2], in_=small[:R, c + 1:c + 2], func=AF.Sqrt), reads=[sr], writes=[sr])
                P.op("dve", lambda h, R=R, c=c: h.reciprocal(out=small[:R, c + 1:c + 2], in_=small[:R, c + 1:c + 2]), reads=[sr], writes=[sr])
                P.op("dve", lambda h, R=R, c=c, hh=hh, o_ap=o_ap, on=on: h.scalar_tensor_tensor(out=on[:R, hh * 256:(hh + 1) * 256], in0=o_ap, scalar=small[:R, c + 1:c + 2], in1=gg_bc[:R, :], op0=ALU.mult, op1=ALU.mult),
                     reads=[("ps", bank), sr, "gsm2"], writes=[("on", k % 2)])
            transposes_to(on, R, 8, 6 + k % 2, lambda: mergedT[:, 0:8, c0:c0 + R], src_res=[("on", k % 2)], dst_res=[("hT", i)], mul_in1=sogT[:, 0:8, c0:c0 + R])
            state_update(i, R, S2)
        P.dma("sp", [lambda h: h.dma_start(out=o_sgla.rearrange("h k v -> k h v"), in_=S2[:])], "osgla", reads=["S2"])

        if stage.get("stop2"):
            reload_x(); return
        fence()
        AB_.reset(markB); AA.reset()
        wuk = AA.alloc("wuk2", [128, 4, 1024], BF16)
        wuv = AA.alloc("wuv2", [128, 4, 1024], BF16)
        kst = [AA.alloc(f"kst2{j}", [128, 1024], BF16) for j in range(2)]
        wuq = AA.alloc("wuq", [128, 4, 1536], BF16)
        kT_s = AB_.alloc("kTs", [128, 8, T], BF16)
        v_s = AB_.alloc("vs", [128, 9, 8, 130], BF16)
        afterKV = AB_.o
        P.op("pool", lambda h: h.memset(v_s[:, :, :, 128:130], 1.0), writes=["vs1"])
        P.dma("pool", [lambda h: h.dma_start(out=wuk[:], in_=w_uk[0].rearrange("(c p) f -> p c f", p=128)),
                       lambda h: h.dma_start(out=wuv[:], in_=w_uv[0].rearrange("(c p) f -> p c f", p=128))], "wukv", writes=["wukv"])
        kv_expand([(1024, 512), (1536, 512), (2048, 32)], range(8, 17), kT_s, v_s, None)
        fence()
        AB_.reset(markB1)
        qnT = AB_.alloc("qnT", [128, 8, T], BF16)
        assert AB_.o <= markB
        AB_.reset(afterKV)
        qpT = AB_.alloc("qpT", [128, 8, T], BF16)
        qpb = [AB_.alloc(f"qpb{j}", [128, 512], BF16) for j in range(2)]
        rt = [AB_.alloc(f"rt{j}", [128, 8, 32], F32) for j in range(2)]
        _sv = AA.o
        AA.reset(AA.hi - 4608)
        PTb = [AA.alloc(f"PT{j}", [128, 512], BF16) for j in range(2)]
        osm = AA.alloc("osm", [128, 1024], BF16)
        obb = [AA.alloc(f"ob{j}", [128, 128], BF16) for j in range(2)]
        AA.reset(_sv)
        P.dma("pool", [lambda h: h.dma_start(out=wuq[:], in_=w_uq[0].rearrange("(c p) f -> p c f", p=128))], "wuq", writes=["wuq"])
        for hh in range(8):
            for tb in range(3):
                c0, n = TB[tb]
                k = uid(); bank = k % 2
                for c in range(4):
                    P.op("pe", lambda h, c=c, c0=c0, n=n, bank=bank, hh=hh: h.matmul(psb[bank][:, :n], lhsT=wuq[:, c, hh * 192:hh * 192 + 128], rhs=cqnT[:, c, c0:c0 + n], start=(c == 0), stop=(c == 3)),
                         reads=["wuq"], writes=[("ps", bank)], signal=(c == 3))
                if k % 2:
                    P.op("act", lambda h, c0=c0, n=n, bank=bank, hh=hh: h.copy(out=qnT[:, hh, c0:c0 + n], in_=psb[bank][:, :n]), reads=[("ps", bank)], writes=["qnT"])
                else:
                    P.op("dve", lambda h, c0=c0, n=n, bank=bank, hh=hh: h.tensor_copy(out=qnT[:, hh, c0:c0 + n], in_=psb[bank][:, :n]), reads=[("ps", bank)], writes=["qnT"])
        for i in range(NT):
            c0, R = tcols(i)
            k = uid(); bank = k % 2
            psv = psb[bank][:R, :].rearrange("p (h e) -> p h e", e=64)
            for c in range(4):
                P.op("pe", lambda h, c=c, c0=c0, R=R, psv=psv: h.matmul(psv, lhsT=cqnT[:, c, c0:c0 + R], rhs=wuq[:, c, :].rearrange("p (h e) -> p h e", e=192)[:, :, 128:192], start=(c == 0), stop=(c == 3)),
                     reads=["wuq"], writes=[("ps", bank)], signal=(c == 3))
            qb_ = qpb[k % 2]
            qv = qb_[:R, :].rearrange("p (h e) -> p h e", e=64)
            cosb = cst[:R, C_COS + 32 * i:C_COS + 32 * i + 32].unsqueeze(1).to_broadcast([R, 8, 32])
            sinb = cst[:R, C_SIN + 32 * i:C_SIN + 32 * i + 32].unsqueeze(1).to_broadcast([R, 8, 32])
            rope(R, psv[:, :, 0:32], psv[:, :, 32:64], cosb, sinb, qv[:, :, 0:32], qv[:, :, 32:64], rt[0][:R], rt[1][:R],
                 [("ps", bank), "cst"], [("qpb", k % 2)], "rt")
            transposes_to(qb_, R, 8, 6 + k % 2, lambda c0=c0, R=R: qpT[0:64, 0:8, c0:c0 + R], eng="act", src_res=[("qpb", k % 2)], dst_res=["qpT"], blk=64)

        SC = 192.0 ** -0.5
        for hh in range(8):
            for kt in range(9):
                Kc = TS if kt == 0 else 128
                kc0 = 0 if kt == 0 else TS + (kt - 1) * 128
                k = uid(); sb_ = k % 2
                P.op("pe", lambda h, hh=hh, Kc=Kc, kc0=kc0, sb_=sb_: h.matmul(psb[sb_][:Kc, :TS], lhsT=kT_s[:, hh, kc0:kc0 + Kc], rhs=qnT[:, hh, TP:T], start=True, stop=False),
                     reads=["kTs", "qnT"], writes=[("ps", sb_)], signal=False)
                P.op("pe", lambda h, hh=hh, Kc=Kc, kc0=kc0, sb_=sb_: h.matmul(psb[sb_][:Kc, :TS], lhsT=kpeT[0:64, TP + kc0:TP + kc0 + Kc], rhs=qpT[0:64, hh, TP:T], start=False, stop=True),
                     reads=["kpeT", "qpT"], writes=[("ps", sb_)])
                PT = PTb[k % 2]
                P.op("act", lambda h, Kc=Kc, sb_=sb_, PT=PT: h.activation(out=PT[:Kc, :TS], in_=psb[sb_][:Kc, :TS], func=AF.Exp, scale=SC), reads=[("ps", sb_)], writes=[("PT", k % 2)])
                P.op("pe", lambda h, hh=hh, Kc=Kc, kt=kt, PT=PT: h.matmul(psb[2][:TS, 0:130], lhsT=PT[:Kc, :TS], rhs=v_s[:Kc, kt, hh, :], start=(kt == 0), stop=(kt == 8)),
                     reads=[("PT", k % 2), "vs", "vs1"], writes=[("ps", 2)], signal=(kt == 8))
            kk = uid(); c = 2 * (kk % 8); sr = ("small", kk % 8)
            P.op("dve", lambda h, c=c: h.reciprocal(out=small[:TS, c:c + 1], in_=psb[2][:TS, 128:129]), reads=[("ps", 2)], writes=[sr])
            P.op("dve", lambda h, c=c, hh=hh: h.tensor_scalar_mul(out=osm[:TS, hh * 128:(hh + 1) * 128], in0=psb[2][:TS, 0:128], scalar1=small[:TS, c:c + 1]), reads=[("ps", 2), sr], writes=["osm"])
        transposes_to(osm, TS, 8, 6, lambda: mergedT[:, 8:16, TP:T], src_res=["osm"], dst_res=[("hT", 8)])

        if stage.get("stop3"):
            reload_x(); return
        fence()
        AA.reset()
        kTh = AA.alloc("kTh", [128, 9 * 1024], BF16)
        vh = AA.alloc("vh", [128, 72, 130], BF16)
        kpeA = AA.alloc("kpeA", [128, 9 * 1024], BF16)
        g1 = gath1.rearrange("(r x) c -> r x c", r=NCORES)
        gv4 = g1[:, 1088:2112, :].rearrange("r (h a) (b d) -> r h (a b) d", h=8, b=8, d=128)
        P.op("pool", lambda h: h.memset(vh[:, :, 128:130], 1.0), writes=["vh1"])
        P.dma("sp", [lambda h: h.dma_start(out=kpeA[0:64, 0:8192].rearrange("d (r c) -> d r c", r=8), in_=g1[:, 1024:1088, :].rearrange("r d c -> d r c")),
                     lambda h: h.dma_start(out=kpeA[0:64, 8192:9216], in_=bounce1[1024:1088, :])], "kpeA", reads=["gath1", "bounce1"], writes=["kpeA"])
        for hh in range(8):
            P.dma("sp", [lambda h, hh=hh: h.dma_start(out=kTh[:, 0:8192].rearrange("d (r c) -> d r c", r=8), in_=g1[:, hh * 128:(hh + 1) * 128, :].rearrange("r d c -> d r c")),
                         lambda h, hh=hh: h.dma_start(out=kTh[:, 8192:9216], in_=bounce1[hh * 128:(hh + 1) * 128, :])], "kTh", reads=["gath1", "bounce1"], writes=["kTh"])
            P.dma("sp", [lambda h, hh=hh, r=r: h.dma_start(out=vh[:, r * 8:(r + 1) * 8, 0:128], in_=gv4[r, hh].rearrange("(j p) d -> p j d", p=128)) for r in range(8)] +
                        [lambda h, hh=hh: h.dma_start(out=vh[:, 64:72, 0:128], in_=bv[hh].rearrange("(j p) d -> p j d", p=128))], "vh", reads=["gath1", "bounce1"], writes=["vh"])
            for qb in range(2):
                q0 = 512 * qb
                for kt in range(72):
                    if kt < 64:
                        cs = 0
                        bias = cst[:, C_RBIAS + kt // 8:C_RBIAS + kt // 8 + 1]
                    else:
                        j = kt - 64
                        cs = max(0, 128 * j - q0)
                        if cs >= 512:
                            continue
                        bias = cst[:, C_RBIAS + 8:C_RBIAS + 9]
                    N = 512 - cs
                    k = uid(); sb_ = k % 2
                    P.op("pe", lambda h, hh=hh, kt=kt, q0=q0, cs=cs, N=N, sb_=sb_: h.matmul(psb[sb_][:, :N], lhsT=kTh[:, kt * 128:(kt + 1) * 128], rhs=qnT[:, hh, q0 + cs:q0 + 512], start=True, stop=False),
                         reads=["kTh", "qnT"], writes=[("ps", sb_)], signal=False)
                    P.op("pe", lambda h, hh=hh, kt=kt, q0=q0, cs=cs, N=N, sb_=sb_: h.matmul(psb[sb_][:, :N], lhsT=kpeA[0:64, kt * 128:(kt + 1) * 128], rhs=qpT[0:64, hh, q0 + cs:q0 + 512], start=False, stop=True),
                         reads=["kpeA", "qpT"], writes=[("ps", sb_)])
                    PT = PTb[k % 2]
                    P.op("act", lambda h, N=N, sb_=sb_, PT=PT, bias=bias: h.activation(out=PT[:, :N], in_=psb[sb_][:, :N], func=AF.Exp, bias=bias, scale=SC), reads=[("ps", sb_), "cst"], writes=[("PT", k % 2)])
                    if kt >= 64 and 128 * (kt - 64) >= q0:
                        P.op("dve", lambda h, PT=PT: h.memset(PT[64:128, 0:64], 0.0), reads=[("PT", k % 2)], writes=[("PT", k % 2)])
                    alist = [a for a in range(4) if 128 * a - cs >= 0]
                    for a in alist:
                        qt = 4 * qb + a
                        qc0 = 128 * a - cs
                        P.op("pe", lambda h, a=a, qc0=qc0, kt=kt, PT=PT, qt=qt: h.matmul(psb[2 + a][:, 0:130], lhsT=PT[:, qc0:qc0 + 128], rhs=vh[:, kt, :], start=(kt == 0), stop=(kt == 64 + qt)),
                             reads=[("PT", k % 2), "vh", "vh1"], writes=[("ps", 2 + a)], signal=(a == alist[-1]))
                for a in range(4):
                    qt = 4 * qb + a
                    kk = uid(); c = 2 * (kk % 8); sr = ("small", kk % 8)
                    ob = obb[kk % 2]
                    P.op("dve", lambda h, c=c, a=a: h.reciprocal(out=small[:, c:c + 1], in_=psb[2 + a][:, 128:129]), reads=[("ps", 2 + a)], writes=[sr])
                    P.op("dve", lambda h, c=c, a=a, ob=ob: h.tensor_scalar_mul(out=ob[:, :], in0=psb[2 + a][:, 0:128], scalar1=small[:, c:c + 1]), reads=[("ps", 2 + a), sr], writes=[("ob", kk % 2)])
                    transposes_to(ob, 128, 1, 6 + kk % 2, lambda hh=hh, qt=qt: mergedT[:, 8 + hh:9 + hh, qt * 128:(qt + 1) * 128], eng="act", src_res=[("ob", kk % 2)], dst_res=[("hT", qt)])

        reload_x()
        AB_.reset()
        wo = [AB_.alloc(f"wo{j}", [128, 8, D], BF16) for j in range(2)]
        wov = w_out[0].rearrange("(kc p) f -> p kc f", p=128)
        P.dma("pool", [lambda h: h.dma_start(out=wo[0][:], in_=wov[:, 0:8, :]), lambda h: h.dma_start(out=wo[1][:], in_=wov[:, 8:16, :])], "wo", writes=["wo"])
        for i in range(NT):
            c0, R = tcols(i)
            for n4 in range(4):
                k = uid(); bank = k % 4
                for kc in range(16):
                    P.op("pe", lambda h, kc=kc, c0=c0, R=R, n4=n4, bank=bank: h.matmul(psb[bank][:R, :], lhsT=mergedT[:, kc, c0:c0 + R], rhs=wo[kc // 8][:, kc % 8, n4 * 512:(n4 + 1) * 512], start=(kc == 0), stop=(kc == 15)),
                         reads=[("hT", i), "wo"], writes=[("ps", bank)], signal=(kc == 15))
                P.op("dve", lambda h, i=i, R=R, n4=n4, bank=bank: h.tensor_tensor(out=x_tm[:R, i, n4 * 512:(n4 + 1) * 512], in0=psb[bank][:R, :], in1=x_tm[:R, i, n4 * 512:(n4 + 1) * 512], op=ALU.add),
                     reads=[("ps", bank)] + xres(i, n4), writes=xres(i, n4))
        fence()

    def ssd_mixer():
        w_in = K.din("w_ssd_in", [1, D, 10304])
        w_conv = K.din("w_ssd_conv", [1, 4, 6144])
        b_conv = K.din("b_ssd_conv", [1, 6144])
        dt_bias = K.din("ssd_dt_bias", [1, 64])
        a_log = K.din("ssd_a_log", [1, 64])
        d_skip = K.din("ssd_d", [1, 64])
        g_norm = K.din("g_ssd_norm", [1, 4096])
        w_out = K.din("w_ssd_out", [1, 4096, D])
        st_ssd = K.din("st_ssd", [64, 64, 128])
        st_conv = K.din("st_conv", [3, 6144])
        o_pssd = K.dout("p_ssd", [64, 64, 128]); o_sssd = K.dout("s_ssd", [64, 64, 128])
        o_pconv = K.dout("p_conv", [3, 6144]); o_sconv = K.dout("s_conv", [3, 6144])
        bounce3 = K.dint("bounce3", [3, 6144]); gath3 = K.dint("gath3", [NCORES * 3, 6144])
        bounce4 = K.dint("bounce4", [128, 4160]); gath4 = K.dint("gath4", [NCORES * 128, 4160])
        ysc = K.dint("ysc", [128, 32, T], BF16)
        winv = w_in[0].rearrange("(dc p) f -> p dc f", p=128)
        CZ, CX, CB_, CC, CDT = 0, 4096, 8192, 9216, 10240

        rmsnorm_hT(g_mix[1:2, :], "mix1")
        spill_x()
        AA.reset(); AB_.reset()
        wA = [AA.alloc(f"swA{s}", [128, 16, 512], BF16) for s in range(2)]
        wcnt = {"n": 0}

        def load_wA(col0, ncols):
            s = wcnt["n"] % 2; wcnt["n"] += 1
            P.dma("pool", [lambda h: h.dma_start(out=wA[s][:, :, :ncols], in_=winv[:, :, col0:col0 + ncols])], ("wA", s), writes=[("wA", s)])
            return s

        BT = AB_.alloc("BT", [128, 8, T], BF16)
        CT = AB_.alloc("CT", [128, 8, T], BF16)
        dt_all = AB_.alloc("dtall", [128, NT, 64], F32)
        cum_all = AB_.alloc("cumall", [128, NT, 64], F32)
        ec_all = AB_.alloc("ecall", [128, NT, 64], F32)
        wdec_all = AB_.alloc("wdall", [128, NT, 64], F32)
        EL_all = AB_.alloc("ELall", [128, NT, 64], F32)
        dtAh = AB_.alloc("dtAh", [128, NT, 64], BF16)
        dtAl = AB_.alloc("dtAl", [128, NT, 64], BF16)
        SS = AB_.alloc("SS", [128, 4096], F32)
        wcT = AB_.alloc("wcT", [128, 48, 8], F32)
        haloT = AB_.alloc("haloT", [128, 48, 6], F32)
        dtb_bc = AB_.alloc("dtbbc", [128, 64], F32)
        a_bc = AB_.alloc("abc", [128, 64], F32)
        dsk_bc = AB_.alloc("dskbc", [128, 64], F32)
        onesb = AB_.alloc("onesb", [128, 128], BF16)
        selb = AB_.alloc("selb", [128, 16], BF16)
        markS = AB_.o
        P.op("pool", lambda h: h.memset(onesb[:], 1.0), writes=["onesb"])
        P.op("dve", lambda h: h.tensor_copy(out=selb[:, 0:14], in_=cst[:, C_SEL:C_SEL + 14]), reads=["cst"], writes=["selb"])
        P.dma("sp", [lambda h: h.dma_start(out=dtb_bc[:], in_=dt_bias[0:1, :].partition_broadcast(128)),
                     lambda h: h.dma_start(out=a_bc[:], in_=a_log[0:1, :].partition_broadcast(128)),
                     lambda h: h.dma_start(out=dsk_bc[:], in_=d_skip[0:1, :].partition_broadcast(128))], "ssm", writes=["ssm"])
        P.op("act", lambda h: h.activation(out=a_bc[:], in_=a_bc[:], func=AF.Exp), reads=["ssm"], writes=["abc"])
        P.op("dve", lambda h: h.tensor_scalar_mul(out=a_bc[:], in0=a_bc[:], scalar1=-1.0), reads=["abc"], writes=["abc"])

        def rows_to_chan(RS, nrow, sel_c0, ncol, dst, ncb, rdres, wrres):
            hi = AA.alloc("rshi", [128, 6144], BF16); lo = AA.alloc("rslo", [128, 6144], BF16)
            W = ncb * 128
            P.op("act", lambda h: h.copy(out=hi[:nrow, :W], in_=RS[:nrow, :W]), reads=rdres, writes=["rshi"])
            P.op("dve", lambda h: h.tensor_tensor(out=lo[:nrow, :W], in0=RS[:nrow, :W], in1=hi[:nrow, :W], op=ALU.subtract), reads=rdres + ["rshi"], writes=["rslo"])
            for cb in range(ncb):
                bank = cb % 2
                P.op("pe", lambda h, cb=cb, bank=bank: h.matmul(psb[bank][:, 0:ncol], lhsT=hi[:nrow, cb * 128:(cb + 1) * 128], rhs=selb[:nrow, sel_c0:sel_c0 + ncol], start=True, stop=False),
                     reads=["rshi", "selb"], writes=[("ps", bank)], signal=False)
                P.op("pe", lambda h, cb=cb, bank=bank: h.matmul(psb[bank][:, 0:ncol], lhsT=lo[:nrow, cb * 128:(cb + 1) * 128], rhs=selb[:nrow, sel_c0:sel_c0 + ncol], start=False, stop=True),
                     reads=["rslo", "selb"], writes=[("ps", bank)])
                P.op("dve", lambda h, cb=cb, bank=bank: h.tensor_copy(out=dst[:, cb, 0:ncol], in_=psb[bank][:, 0:ncol]), reads=[("ps", bank)], writes=wrres)

        mA = AA.o
        AA.reset(AA.lo)
        RS = AA.alloc("RS", [128, 6144], F32)
        P.op("pool", lambda h: h.memset(RS[0:32, :], 0.0), writes=["RS"])
        P.dma("sp", [lambda h: h.dma_start(out=RS[0:4, :], in_=w_conv[0]), lambda h: h.dma_start(out=RS[4:5, :], in_=b_conv[0:1, :]),
                     lambda h: h.dma_start(out=RS[5:6, 0:4096], in_=g_norm[0:1, :])], "rs1", reads=["RS"], writes=["RS"])
        rows_to_chan(RS, 8, 6, 8, wcT, 48, ["RS"], ["wcT"])
        fence()
        AA.reset(mA)
        tail = AA.alloc("tail", [128, 6144], F32)
        tail2 = AB_.alloc("tail2", [128, 6144], F32)
        for gi in range(12):
            s = load_wA(CX + 512 * gi, 512)
            for (tl, cc0, bank) in ((tail, TP - 3, gi % 2), (tail2, T - 3, 2 + gi % 2)):
                for dc in range(16):
                    P.op("pe", lambda h, dc=dc, bank=bank, s=s, cc0=cc0: h.matmul(psb[bank][0:3, :], lhsT=hT[:, dc, cc0:cc0 + 3], rhs=wA[s][:, dc, :], start=(dc == 0), stop=(dc == 15)),
                         reads=[("hT", 7), ("hT", 8), ("wA", s)], writes=[("ps", bank)], signal=(dc == 15))
                P.op("dve", lambda h, gi=gi, bank=bank, tl=tl: h.tensor_copy(out=tl[0:3, gi * 512:(gi + 1) * 512], in_=psb[bank][0:3, :]), reads=[("ps", bank)], writes=["tail"])
        P.dma("sp", [lambda h: h.dma_start(out=o_pconv[:, :], in_=tail[0:3, :]), lambda h: h.dma_start(out=o_sconv[:, :], in_=tail2[0:3, :]),
                     lambda h: h.dma_start(out=bounce3[:, :], in_=tail[0:3, :])], "tailo", reads=["tail"], writes=["bounce3"])
        allgather(bounce3, gath3, "ag3", ["bounce3"], ["gath3"])
        fence()
        AB_.reset(markS)
        AA.reset(AA.lo)
        RS2 = AA.alloc("RS2", [128, 6144], F32)
        P.op("pool", lambda h: h.memset(RS2[0:32, :], 0.0), writes=["RS2"])
        P.dma("sp", [lambda h: h.dma_start(out=RS2[0:24, :], in_=gath3[:, :]), lambda h: h.dma_start(out=RS2[24:27, :], in_=st_conv[:, :])], "rs2", reads=["gath3", "RS2"], writes=["RS2"])
        rows_to_chan(RS2, 27, 0, 6, haloT, 48, ["RS2"], ["haloT"])
        fence()
        AA.reset(mA)

        xpre = [AA.alloc(f"xpre{j}", [128, 1064], F32) for j in range(2)]
        acc = AA.alloc("acc", [128, T], F32)
        xs4 = AA.alloc("xs4", [128, 4, T], BF16)
        x_g = AA.alloc("xg", [128, NT, 512], BF16)
        sz_g = AA.alloc("szg", [128, NT, 512], BF16)
        assert AA.o <= AA.hi

        def proj_conv_block(s, mb, cb, dst_fn):
            k = uid(); xp_ = xpre[k % 2]; xr = ("xpre", k % 2)
            P.op("dve", lambda h: h.tensor_copy(out=xp_[:, 0:3], in_=haloT[:, cb, 0:3]), reads=["haloT"], writes=[xr])
            P.op("dve", lambda h: h.tensor_copy(out=xp_[:, 1027:1030], in_=haloT[:, cb, 3:6]), reads=["haloT"], writes=[xr])
            for tb in range(3):
                c0, n = TB[tb]
                bank = tb % 2
                for dc in range(16):
                    P.op("pe", lambda h, dc=dc, c0=c0, n=n, bank=bank: h.matmul(psb[bank][:, :n], lhsT=wA[s][:, dc, mb * 128:(mb + 1) * 128], rhs=hT[:, dc, c0:c0 + n], start=(dc == 0), stop=(dc == 15)),
                         reads=tb_tiles(tb) + [("wA", s)], writes=[("ps", bank)], signal=(dc == 15))
                if tb < 2:
                    P.op("act", lambda h, c0=c0, n=n, bank=bank: h.copy(out=xp_[:, 3 + c0:3 + c0 + n], in_=psb[bank][:, :n]), reads=[("ps", bank)], writes=[xr])
                else:
                    P.op("act", lambda h, c0=c0, bank=bank: h.copy(out=xp_[:, 3 + c0:3 + TP], in_=psb[bank][:, 0:TP - c0]), reads=[("ps", bank)], writes=[xr])
                    P.op("act", lambda h, c0=c0, bank=bank: h.copy(out=xp_[:, 1030:1030 + TS], in_=psb[bank][:, TP - c0:TP - c0 + TS]), reads=[("ps", bank)], writes=[xr])
            for (o0, L, a0) in ((0, TP, 0), (1027, TS, TP)):
                P.op("dve", lambda h, o0=o0, L=L, a0=a0: h.tensor_scalar(out=acc[:, a0:a0 + L], in0=xp_[:, o0 + 3:o0 + 3 + L], scalar1=wcT[:, cb, 3:4], scalar2=wcT[:, cb, 4:5], op0=ALU.mult, op1=ALU.add),
                     reads=[xr, "wcT"], writes=["acc"])
                for j in range(3):
                    P.op("dve", lambda h, o0=o0, L=L, a0=a0, j=j: h.scalar_tensor_tensor(out=acc[:, a0:a0 + L], in0=xp_[:, o0 + j:o0 + j + L], scalar=wcT[:, cb, j:j + 1], in1=acc[:, a0:a0 + L], op0=ALU.mult, op1=ALU.add),
                         reads=[xr, "wcT", "acc"], writes=["acc"])
            dst, dres = dst_fn()
            P.op("act", lambda h: h.activation(out=dst, in_=acc[:, :], func=AF.Silu), reads=["acc"], writes=dres)

        for gi in range(4):
            s = load_wA(CB_ + 512 * gi, 512)
            for mb in range(4):
                blk = gi * 4 + mb
                tgt = BT if blk < 8 else CT
                proj_conv_block(s, mb, 32 + blk, lambda tgt=tgt, blk=blk: (tgt[:, blk % 8, :], ["BT" if blk < 8 else "CT"]))
        s = load_wA(CDT, 64)
        dtf = [AB_.alloc(f"dtf{j}", [128, 64], F32) for j in range(2)]
        for i in range(NT):
            c0, R = tcols(i)
            k = uid(); bank = k % 2; tf = dtf[k % 2]; tr = ("dtf", k % 2)
            for dc in range(16):
                P.op("pe", lambda h, dc=dc, c0=c0, R=R, bank=bank: h.matmul(psb[bank][:R, 0:64], lhsT=hT[:, dc, c0:c0 + R], rhs=wA[s][:, dc, 0:64], start=(dc == 0), stop=(dc == 15)),
                     reads=[("hT", i), ("wA", s)], writes=[("ps", bank)], signal=(dc == 15))
            P.op("dve", lambda h, R=R, bank=bank, tf=tf: h.tensor_tensor(out=tf[:R, :], in0=psb[bank][:R, 0:64], in1=dtb_bc[:R, :], op=ALU.add), reads=[("ps", bank), "ssm"], writes=[tr])
            P.op("act", lambda h, R=R, tf=tf: h.activation(out=tf[:R, :], in_=tf[:R, :], func=AF.Exp), reads=[tr], writes=[tr])
            P.op("act", lambda h, R=R, i=i, tf=tf: h.activation(out=dt_all[:R, i, :], in_=tf[:R, :], func=AF.Ln, bias=cst[:R, C_ONE:C_ONE + 1], scale=1.0), reads=[tr, "cst"], writes=["dtall"])
            P.op("dve", lambda h, R=R, i=i, tf=tf: h.tensor_tensor(out=tf[:R, :], in0=dt_all[:R, i, :], in1=a_bc[:R, :], op=ALU.mult), reads=["dtall", "abc"], writes=[tr])
            P.op("act", lambda h, R=R, i=i, tf=tf: h.copy(out=dtAh[:R, i, :], in_=tf[:R, :]), reads=[tr], writes=["dtAh"])
            P.op("dve", lambda h, R=R, i=i, tf=tf: h.tensor_tensor(out=dtAl[:R, i, :], in0=tf[:R, :], in1=dtAh[:R, i, :], op=ALU.subtract), reads=[tr, "dtAh"], writes=["dtAl"])
            b1, b2 = 2, 3
            for (src, st_, sp_) in ((dtAh, True, False), (dtAl, False, True)):
                P.op("pe", lambda h, R=R, i=i, src=src, st_=st_, sp_=sp_: h.matmul(psb[b1][:R, 0:64], lhsT=cstb[:R, 0:R], rhs=src[:R, i, :], start=st_, stop=sp_), reads=["dtAh", "dtAl", "cstb"], writes=[("ps", b1)], signal=sp_)
            for (src, st_, sp_) in ((dtAh, True, False), (dtAl, False, True)):
                P.op("pe", lambda h, R=R, i=i, src=src, st_=st_, sp_=sp_: h.matmul(psb[b2][:, 0:64], lhsT=onesb[:R, :], rhs=src[:R, i, :], start=st_, stop=sp_), reads=["dtAh", "dtAl", "onesb"], writes=[("ps", b2)], signal=sp_)
            P.op("dve", lambda h, R=R, i=i: h.tensor_scalar_mul(out=cum_all[:R, i, :], in0=psb[b1][:R, 0:64], scalar1=-16.0), reads=[("ps", b1)], writes=["cumall"])
            P.op("act", lambda h, R=R, i=i: h.activation(out=ec_all[:R, i, :], in_=cum_all[:R, i, :], func=AF.Exp), reads=["cumall"], writes=["ecall"])
            P.op("act", lambda h, i=i: h.activation(out=EL_all[:, i, :], in_=psb[b2][:, 0:64], func=AF.Exp), reads=[("ps", b2)], writes=["ELall"])
            P.op("dve", lambda h, R=R, i=i, tf=tf: h.tensor_tensor(out=tf[:R, :], in0=psb[b2][:R, 0:64], in1=cum_all[:R, i, :], op=ALU.subtract), reads=[("ps", b2), "cumall"], writes=[tr])
            P.op("act", lambda h, R=R, tf=tf: h.activation(out=tf[:R, :], in_=tf[:R, :], func=AF.Exp), reads=[tr], writes=[tr])
            P.op("dve", lambda h, R=R, i=i, tf=tf: h.tensor_tensor(out=wdec_all[:R, i, :], in0=tf[:R, :], in1=dt_all[:R, i, :], op=ALU.mult), reads=[tr, "dtall"], writes=["wdall"])

        Btm = AB_.alloc("Btm", [128, 128], BF16)
        xw = AB_.alloc("xw", [128, 512], BF16)
        Sbf = AB_.alloc("Sbfg", [128, 512], BF16)
        Rmh = AB_.alloc("Rmh", [128, 8, 128], BF16); Rml = AB_.alloc("Rml", [128, 8, 128], BF16)
        dm = AB_.alloc("dm", [128, 8, 128], F32)
        wT = AB_.alloc("wT", [128, 8, 128], BF16)
        CBm = AB_.alloc("CBm", [128, 128], F32)
        ysb = AB_.alloc("ysb", [128, 512], F32)
        ynb = AB_.alloc("ynb", [128, 512], BF16)
        yTs = [AB_.alloc(f"yTs{j}", [128, 4, 128], BF16) for j in range(2)]
        stf = AB_.alloc("stf", [128, 4, 128], F32)
        sth = AB_.alloc("sth", [128, 512], BF16); stl = AB_.alloc("stl", [128, 512], BF16)

        def x_group(g, with_z):
            s = load_wA(CX + 512 * g, 512)
            for mb in range(4):
                proj_conv_block(s, mb, 4 * g + mb, lambda mb=mb: (xs4[:, mb, :], ["xs4"]))
            for i in range(NT):
                c0, R = tcols(i)
                k = uid(); bank = 6 + k % 2
                pv = ps_bf(bank)[:, 0:512].rearrange("p (q c) -> p q c", c=128)
                for q in range(4):
                    P.op("pe", lambda h, q=q, c0=c0, R=R, pv=pv: h.transpose(out=pv[:R, q, :], in_=xs4[:, q, c0:c0 + R], identity=ident_bf[:, :]), reads=["xs4", "identb"], writes=[("ps", bank)], signal=(q == 3))
                P.op("act", lambda h, i=i, R=R, pv=pv: h.copy(out=x_g[:R, i, :], in_=pv[:R, :, :]), reads=[("ps", bank)], writes=["xg"])
            if with_z:
                s2 = load_wA(CZ + 512 * g, 512)
                for i in range(NT):
                    c0, R = tcols(i)
                    bank = i % 2
                    for dc in range(16):
                        P.op("pe", lambda h, dc=dc, c0=c0, R=R, bank=bank: h.matmul(psb[bank][:R, :], lhsT=hT[:, dc, c0:c0 + R], rhs=wA[s2][:, dc, :], start=(dc == 0), stop=(dc == 15)),
                             reads=[("hT", i), ("wA", s2)], writes=[("ps", bank)], signal=(dc == 15))
                    P.op("act", lambda h, i=i, R=R, bank=bank: h.activation(out=sz_g[:R, i, :], in_=psb[bank][:R, :], func=AF.Silu), reads=[("ps", bank)], writes=["szg"])

        def hb(ap2, R, g):
            return ap2[:R, 8 * g:8 * g + 8].unsqueeze(2).to_broadcast([R, 8, 64])

        def state_step(g, i, R):
            c0 = i * 128
            Sg = SS[:, 512 * g:512 * (g + 1)]
            pvb = ps_bf(5)[:, 0:128]
            P.op("pe", lambda h: h.transpose(out=pvb[:R, :], in_=BT[:, g, c0:c0 + R], identity=ident_bf[:, :]), reads=["BT", "identb"], writes=[("ps", 5)])
            P.op("act", lambda h: h.copy(out=Btm[:R, :], in_=pvb[:R, :]), reads=[("ps", 5)], writes=["Btm"])
            P.op("dve", lambda h: h.tensor_tensor(out=xw[:R, :].rearrange("p (a b) -> p a b", b=64), in0=x_g[:R, i, :].rearrange("p (a b) -> p a b", b=64), in1=hb(wdec_all[:, i, :], R, g), op=ALU.mult),
                 reads=["xg", "wdall"], writes=["xw"])
            P.op("pe", lambda h: h.matmul(psb[4][:, :], lhsT=Btm[:R, :], rhs=xw[:R, :], start=True, stop=True), reads=["Btm", "xw"], writes=[("ps", 4)])
            P.op("dve", lambda h: h.tensor_tensor(out=Sg.rearrange("p (a b) -> p a b", b=64), in0=Sg.rearrange("p (a b) -> p a b", b=64), in1=hb(EL_all[:, i, :], 128, g), op=ALU.mult), reads=["SS", "ELall"], writes=["SS"])
            P.op("dve", lambda h: h.tensor_tensor(out=Sg, in0=Sg, in1=psb[4][:, :], op=ALU.add), reads=["SS", ("ps", 4)], writes=["SS"])

        P.op("pool", lambda h: h.memset(SS[:], 0.0), writes=["SS"])
        for g in range(8):
            x_group(g, False)
            for i in range(8):
                state_step(g, i, 128)
        dtot = AB_.alloc("dtot", [128, 64], F32)
        P.op("dve", lambda h: h.tensor_copy(out=dtot[:], in_=EL_all[:, 0, :]), reads=["ELall"], writes=["dtot"])
        for i in range(1, 8):
            P.op("dve", lambda h, i=i: h.tensor_tensor(out=dtot[:], in0=dtot[:], in1=EL_all[:, i, :], op=ALU.mult), reads=["ELall", "dtot"], writes=["dtot"])
        P.dma("sp", [lambda h: h.dma_start(out=bounce4[:, 0:4096], in_=SS[:]), lambda h: h.dma_start(out=bounce4[:, 4096:4160], in_=dtot[:])], "b4", reads=["SS", "dtot"], writes=["bounce4"])
        allgather(bounce4, gath4, "ag4", ["bounce4"], ["gath4"])
        fence()
        mA2 = AA.o
        AA.reset(AA.lo)
        Gr = AA.alloc("Gr", [128, 4160], F32)
        fd = AB_.alloc("fd", [128, 64], F32)
        P.op("pool", lambda h: h.memset(SS[:], 0.0), writes=["SS"])
        for r in range(7):
            m = cst[:, C_CMASK + r:C_CMASK + r + 1]
            P.dma("sp", [lambda h, r=r: h.dma_start(out=Gr[:], in_=gath4[r * 128:(r + 1) * 128, :])], "grl", reads=["gath4"], writes=["Gr"])
            P.op("dve", lambda h, m=m: h.tensor_scalar(out=fd[:], in0=Gr[:, 4096:4160], scalar1=-1.0, scalar2=m, op0=ALU.add, op1=ALU.mult), reads=["Gr", "cst"], writes=["fd"])
            P.op("dve", lambda h: h.tensor_scalar_add(out=fd[:], in0=fd[:], scalar1=1.0), reads=["fd"], writes=["fd"])
            S3 = SS[:].rearrange("p (a b) -> p a b", b=64)
            P.op("dve", lambda h, S3=S3: h.tensor_tensor(out=S3, in0=S3, in1=fd[:, :].unsqueeze(2).to_broadcast([128, 64, 64]), op=ALU.mult), reads=["fd", "SS"], writes=["SS"])
            P.op("dve", lambda h, m=m: h.scalar_tensor_tensor(out=SS[:], in0=Gr[:, 0:4096], scalar=m, in1=SS[:], op0=ALU.mult, op1=ALU.add), reads=["Gr", "SS"], writes=["SS"])
        fence()
        AA.reset(mA2)

        def state_io_out(g, dst):
            Sg = SS[:, 512 * g:512 * (g + 1)]
            P.op("act", lambda h: h.copy(out=sth[:, :], in_=Sg), reads=["SS"], writes=["sth"])
            P.op("dve", lambda h: h.tensor_tensor(out=stl[:, :], in0=Sg, in1=sth[:, :], op=ALU.subtract), reads=["SS", "sth"], writes=["stl"])
            pvh = ps_bf(6)[:, 0:512].rearrange("p (q c) -> p q c", c=128)
            pvl = ps_bf(7)[:, 0:512].rearrange("p (q c) -> p q c", c=128)
            for q in range(4):
                P.op("pe", lambda h, q=q: h.transpose(out=pvh[:, q, :], in_=sth[:, q * 128:(q + 1) * 128], identity=ident_bf[:, :]), reads=["sth", "identb"], writes=[("ps", 6)], signal=(q == 3))
            for q in range(4):
                P.op("pe", lambda h, q=q: h.transpose(out=pvl[:, q, :], in_=stl[:, q * 128:(q + 1) * 128], identity=ident_bf[:, :]), reads=["stl", "identb"], writes=[("ps", 7)], signal=(q == 3))
            P.op("act", lambda h: h.copy(out=stf[:], in_=pvh), reads=[("ps", 6)], writes=["stf"])
            P.op("dve", lambda h: h.tensor_tensor(out=stf[:], in0=stf[:], in1=pvl, op=ALU.add), reads=["stf", ("ps", 7)], writes=["stf"])
            dv = dst.rearrange("h p n -> (h p) n")[512 * g:512 * (g + 1), :].rearrange("(q r) n -> r q n", r=128)
            P.dma("sp", [lambda h: h.dma_start(out=dv, in_=stf[:])], ("sto", g % 2), reads=["stf"])

        def state_io_in(g, src):
            Sg = SS[:, 512 * g:512 * (g + 1)]
            sv = src.rearrange("h p n -> (h p) n")[512 * g:512 * (g + 1), :].rearrange("(q r) n -> r q n", r=128)
            P.dma("sp", [lambda h: h.dma_start(out=stf[:], in_=sv)], "sti", writes=["stf"])
            s2 = stf[:].rearrange("p q n -> p (q n)")
            P.op("act", lambda h: h.copy(out=sth[:, :], in_=s2), reads=["stf"], writes=["sth"])
            P.op("dve", lambda h: h.tensor_tensor(out=stl[:, :], in0=s2, in1=sth[:, :], op=ALU.subtract), reads=["stf", "sth"], writes=["stl"])
            pvh = ps_bf(6)[:, 0:512].rearrange("p (q c) -> p q c", c=128)
            pvl = ps_bf(7)[:, 0:512].rearrange("p (q c) -> p q c", c=128)
            for q in range(4):
                P.op("pe", lambda h, q=q: h.transpose(out=pvh[:, q, :], in_=sth[:, q * 128:(q + 1) * 128], identity=ident_bf[:, :]), reads=["sth", "identb"], writes=[("ps", 6)], signal=(q == 3))
            for q in range(4):
                P.op("pe", lambda h, q=q: h.transpose(out=pvl[:, q, :], in_=stl[:, q * 128:(q + 1) * 128], identity=ident_bf[:, :]), reads=["stl", "identb"], writes=[("ps", 7)], signal=(q == 3))
            P.op("act", lambda h: h.copy(out=Sg.rearrange("p (q c) -> p q c", c=128), in_=pvh), reads=[("ps", 6)], writes=["SS"])
            P.op("dve", lambda h: h.tensor_tensor(out=Sg.rearrange("p (q c) -> p q c", c=128), in0=Sg.rearrange("p (q c) -> p q c", c=128), in1=pvl, op=ALU.add), reads=["SS", ("ps", 7)], writes=["SS"])

        for g in range(8):
            x_group(g, True)
            Sg = SS[:, 512 * g:512 * (g + 1)]
            for i in range(NT):
                c0, R = tcols(i)
                if i == 8:
                    state_io_out(g, o_pssd)
                    state_io_in(g, st_ssd)
                trb = cst[:R, C_TRII:C_TRII + R].unsqueeze(1).to_broadcast([R, 8, R])
                P.op("dve", lambda h, i=i, R=R, trb=trb: h.scalar_tensor_tensor(out=Rmh[:R, :, :R], in0=trb, scalar=-16.0, in1=dtAh[:R, i, 8 * g:8 * g + 8].unsqueeze(2).to_broadcast([R, 8, R]), op0=ALU.mult, op1=ALU.mult), reads=["cst", "dtAh"], writes=["Rmh"])
                P.op("dve", lambda h, i=i, R=R, trb=trb: h.scalar_tensor_tensor(out=Rml[:R, :, :R], in0=trb, scalar=-16.0, in1=dtAl[:R, i, 8 * g:8 * g + 8].unsqueeze(2).to_broadcast([R, 8, R]), op0=ALU.mult, op1=ALU.mult), reads=["cst", "dtAl"], writes=["Rml"])
                for half in range(2):
                    bk = 2 + half
                    ov = psb[bk][:R, 0:4 * R].rearrange("p (a b) -> p a b", b=R)
                    P.op("pe", lambda h, R=R, ov=ov, half=half: h.matmul(ov, lhsT=onesb[:R, :R], rhs=Rmh[:R, 4 * half:4 * half + 4, :R], start=True, stop=False), reads=["Rmh", "onesb"], writes=[("ps", bk)], signal=False)
                    P.op("pe", lambda h, R=R, ov=ov, half=half: h.matmul(ov, lhsT=onesb[:R, :R], rhs=Rml[:R, 4 * half:4 * half + 4, :R], start=False, stop=True), reads=["Rml", "onesb"], writes=[("ps", bk)])
                    P.op("dve", lambda h, i=i, R=R, ov=ov, half=half: h.tensor_tensor(out=dm[:R, 4 * half:4 * half + 4, :R], in0=ov, in1=cum_all[:R, i, 8 * g + 4 * half:8 * g + 4 * half + 4].unsqueeze(2).to_broadcast([R, 4, R]), op=ALU.subtract),
                         reads=[("ps", bk), "cumall"], writes=["dm"])
                P.op("dve", lambda h, R=R: h.tensor_scalar_min(out=dm[:R, :, :R], in0=dm[:R, :, :R], scalar1=0.0), reads=["dm"], writes=["dm"])
                P.op("act", lambda h, R=R: h.activation(out=dm[:R, :, :R], in_=dm[:R, :, :R], func=AF.Exp), reads=["dm"], writes=["dm"])
                P.op("pe", lambda h, c0=c0, R=R: h.matmul(psb[4][:R, :R], lhsT=BT[:, g, c0:c0 + R], rhs=CT[:, g, c0:c0 + R], start=True, stop=True), reads=["BT", "CT"], writes=[("ps", 4)])
                P.op("dve", lambda h, R=R: h.scalar_tensor_tensor(out=CBm[:R, :R], in0=psb[4][:R, :R], scalar=-16.0, in1=cst[:R, C_TRII:C_TRII + R], op0=ALU.mult, op1=ALU.mult), reads=[("ps", 4), "cst"], writes=["CBm"])
                P.op("dve", lambda h, R=R: h.tensor_tensor(out=dm[:R, :, :R], in0=dm[:R, :, :R], in1=CBm[:R, :R].unsqueeze(1).to_broadcast([R, 8, R]), op=ALU.mult), reads=["dm", "CBm"], writes=["dm"])
                P.op("dve", lambda h, i=i, R=R: h.tensor_tensor(out=wT[:R, :, :R], in0=dm[:R, :, :R], in1=dt_all[:R, i, 8 * g:8 * g + 8].unsqueeze(2).to_broadcast([R, 8, R]), op=ALU.mult), reads=["dm", "dtall"], writes=["wT"])
                P.op("act", lambda h, Sg=Sg: h.copy(out=Sbf[:, :], in_=Sg), reads=["SS"], writes=["Sbfg"])
                P.op("pe", lambda h, c0=c0, R=R: h.matmul(psb[0][:R, :], lhsT=CT[:, g, c0:c0 + R], rhs=Sbf[:, :], start=True, stop=True), reads=["CT", "Sbfg"], writes=[("ps", 0)])
                for hh in range(8):
                    P.op("pe", lambda h, hh=hh, i=i, R=R: h.matmul(psb[1][:R, 64 * hh:64 * hh + 64], lhsT=wT[:R, hh, :R], rhs=x_g[:R, i, 64 * hh:64 * hh + 64], start=True, stop=True), reads=["wT", "xg"], writes=[("ps", 1)], signal=(hh == 7))
                y3 = ysb[:R, :].rearrange("p (a b) -> p a b", b=64)
                P.op("dve", lambda h, i=i, R=R, y3=y3: h.tensor_tensor(out=y3, in0=psb[0][:R, :].rearrange("p (a b) -> p a b", b=64), in1=hb(ec_all[:, i, :], R, g), op=ALU.mult), reads=[("ps", 0), "ecall"], writes=["ysb"])
                P.op("dve", lambda h, R=R: h.tensor_tensor(out=ysb[:R, :], in0=ysb[:R, :], in1=psb[1][:R, :], op=ALU.add), reads=["ysb", ("ps", 1)], writes=["ysb"])
                xv3 = xw[:R, :].rearrange("p (a b) -> p a b", b=64)
                P.op("dve", lambda h, i=i, R=R, xv3=xv3: h.tensor_tensor(out=xv3, in0=x_g[:R, i, :].rearrange("p (a b) -> p a b", b=64), in1=hb(dsk_bc, R, g), op=ALU.mult), reads=["xg", "ssm"], writes=["xw"])
                P.op("dve", lambda h, R=R: h.tensor_tensor(out=ysb[:R, :], in0=ysb[:R, :], in1=xw[:R, :], op=ALU.add), reads=["ysb", "xw"], writes=["ysb"])
                P.op("dve", lambda h, i=i, R=R: h.tensor_tensor(out=ysb[:R, :], in0=ysb[:R, :], in1=sz_g[:R, i, :], op=ALU.mult), reads=["ysb", "szg"], writes=["ysb"])
                kk = uid(); c = 2 * (kk % 8); sr = ("small", kk % 8)
                P.op("act", lambda h, R=R, c=c: h.activation(out=junk[:R, :512], in_=ysb[:R, :], func=AF.Square, accum_out=small[:R, c:c + 1]), reads=["ysb"], writes=["junk", sr])
                P.op("dve", lambda h, R=R, c=c: h.tensor_scalar(out=small[:R, c + 1:c + 2], in0=small[:R, c:c + 1], scalar1=1.0 / 512, scalar2=EPS, op0=ALU.mult, op1=ALU.add), reads=[sr], writes=[sr])
                P.op("act", lambda h, R=R, c=c: h.activation(out=small[:R, c + 1:c + 2], in_=small[:R, c + 1:c + 2], func=AF.Sqrt), reads=[sr], writes=[sr])
                P.op("dve", lambda h, R=R, c=c: h.reciprocal(out=small[:R, c + 1:c + 2], in_=small[:R, c + 1:c + 2]), reads=[sr], writes=[sr])
                P.op("dve", lambda h, R=R, c=c: h.tensor_scalar_mul(out=ynb[:R, :], in0=ysb[:R, :], scalar1=small[:R, c + 1:c + 2]), reads=["ysb", sr], writes=["ynb"])
                yt = yTs[kk % 2]
                transposes_to(ynb, R, 4, 6 + kk % 2, lambda yt=yt, R=R: yt[:, :, :R], src_res=["ynb"], dst_res=[("yTs", kk % 2)],
                              mul_in1=wcT[:, 4 * g:4 * g + 4, 5:6].to_broadcast([128, 4, R]))
                P.dma("sp", [lambda h, yt=yt, c0=c0, R=R: h.dma_start(out=ysc[:, 4 * g:4 * g + 4, c0:c0 + R], in_=yt[:, :, :R])], ("yso", kk % 2), reads=[("yTs", kk % 2)], writes=["ysc"])
                state_step(g, i, R)
            state_io_out(g, o_sssd)

        reload_x()
        AB_.reset()
        yT = AB_.alloc("yT", [128, 32, T], BF16)
        wo = [AB_.alloc(f"swo{j}", [128, 4, D], BF16) for j in range(1)]
        P.dma("sp", [lambda h: h.dma_start(out=yT[:], in_=ysc[:, :, :])], "ytl", reads=["ysc"], writes=["yT"])
        wov = w_out[0].rearrange("(kc p) f -> p kc f", p=128)
        hw = [K.sb(f"hwo{j}", [128, 4, D], BF16, SB_BASE + NT * D * 4 + j * 16384) for j in range(2)]
        for kg in range(8):
            P.dma("pool", [lambda h, kg=kg: h.dma_start(out=hw[kg % 2][:], in_=wov[:, 4 * kg:4 * kg + 4, :])], ("hwo", kg % 2), writes=[("hwo", kg % 2)] + [("hT", i) for i in range(NT)])
            for i in range(NT):
                c0, R = tcols(i)
                for n4 in range(4):
                    k = uid(); bank = k % 4
                    for kc in range(4):
                        P.op("pe", lambda h, kc=kc, c0=c0, R=R, n4=n4, bank=bank, kg=kg: h.matmul(psb[bank][:R, :], lhsT=yT[:, 4 * kg + kc, c0:c0 + R], rhs=hw[kg % 2][:, kc, n4 * 512:(n4 + 1) * 512], start=(kc == 0), stop=(kc == 3)),
                             reads=["yT", ("hwo", kg % 2)], writes=[("ps", bank)], signal=(kc == 3))
                    P.op("dve", lambda h, i=i, R=R, n4=n4, bank=bank: h.tensor_tensor(out=x_tm[:R, i, n4 * 512:(n4 + 1) * 512], in0=psb[bank][:R, :], in1=x_tm[:R, i, n4 * 512:(n4 + 1) * 512], op=ALU.add),
                         reads=[("ps", bank)] + xres(i, n4), writes=xres(i, n4))
        fence()

    cst_in = K.din("cst_in", [128, CW])
    P.dma("sp", [lambda h: h.dma_start(out=cst[:], in_=cst_in[:, :])], "cst", writes=["cst"])
    P.op("dve", lambda h: h.tensor_copy(out=cstb[:, 0:256], in_=cst[:, C_TRII:C_TRII + 256]), reads=["cst"], writes=["cstb"])
    P.op("dve", lambda h: h.tensor_copy(out=cstb[:, 256:258], in_=cst[:, C_N16:C_N16 + 2]), reads=["cst"], writes=["cstb"])
    for layer in range(2):
        if not stage.get("noffn"):
            rmsnorm_hT(g_ffn1[layer:layer + 1, :], "f1")
            ffn(layer, wf["w_ffn1_gate"], wf["w_ffn1_up"], wf["w_ffn1_down"])
            fence()
        if layer == 0:
            ab_mixer()
            if stage.get("l0only"):
                break
        else:
            if not stage.get("nossd"):
                ssd_mixer()
        if not stage.get("noffn"):
            rmsnorm_hT(g_ffn2[layer:layer + 1, :], "f2")
            ffn(layer, wf["w_ffn2_gate"], wf["w_ffn2_up"], wf["w_ffn2_down"])
            fence()
    final_out()
    stats = P.build()
    print("PROGRAM", stats, flush=True)
    return nc, stats, sorted(K.dram.keys())


_CACHE = {}


def make_consts(c):
    cs = np.zeros((128, CW), np.float32)
    sidx = np.arange(128)[:, None]
    tidx = np.arange(128)[None, :]
    cs[:, C_TRII:C_TRII + 128] = np.where(sidx <= tidx, -1.0 / 16, 0.0)
    cs[:, C_TRIX:C_TRIX + 128] = np.where(sidx > tidx, -1.0 / 16, 0.0)
    inv = (np.float32(10000.0) ** (-(np.arange(32, dtype=np.float32)) / np.float32(32))).astype(np.float32)
    for i in range(NT):
        if i < 8:
            pos = (c * TP + 128 * i + np.arange(128)).astype(np.float32)
        else:
            pos = (1024 + np.arange(128)).astype(np.float32)
        ang = (pos[:, None] * inv[None, :]).astype(np.float32)
        cs[:, C_COS + 32 * i:C_COS + 32 * i + 32] = np.cos(ang)
        cs[:, C_SIN + 32 * i:C_SIN + 32 * i + 32] = np.sin(ang)
    for r in range(8):
        cs[:, C_CMASK + r] = 1.0 if r < c else 0.0
        cs[:, C_RBIAS + r] = 0.0 if r < c else -30000.0
    cs[:, C_RBIAS + 8] = 0.0
    cs[:, C_N16:C_N16 + 2] = -1.0 / 16
    cs[:, C_ONE] = 1.0
    for j in range(3):
        if c > 0:
            cs[3 * (c - 1) + j, C_SEL + j] = 1.0
        cs[24 + j, C_SEL + 3 + j] = 1.0
    for j in range(8):
        cs[j, C_SEL + 6 + j] = 1.0
    return cs


def kernel(**inputs):
    r, names = _run({}, inputs)
    return _assemble(r, names, strict=True)


def _run(stage, inputs):
    key = repr(sorted(stage.items()))
    if key not in _CACHE:
        _CACHE[key] = build_program(stage)
    nc, stats, names = _CACHE[key]
    f = lambda a: np.ascontiguousarray(np.asarray(a, dtype=np.float32))
    x_prompt = f(inputs["x_prompt"])
    x_sample = f(inputs["x_sample"])
    shared = {}
    for nm in names:
        if nm in inputs and nm not in ("x_prompt", "x_sample"):
            shared[nm] = f(inputs[nm])
    shared["g_final"] = f(inputs["g_final"]).reshape(1, D)
    in_maps = []
    for c in range(NCORES):
        m = dict(shared)
        m["xp"] = x_prompt[0, c * TP:(c + 1) * TP, :]
        m["xs"] = x_sample[c]
        m["cst_in"] = make_consts(c)
        if "c_ckv" in names:
            m["c_ckv"] = f(inputs["cache_mla_ckv"])[0, c]
            m["c_kpe"] = f(inputs["cache_mla_kpe"])[0, c]
            m["st_gla"] = f(inputs["state_gla"])[0, c]
        if "st_ssd" in names:
            m["st_ssd"] = f(inputs["state_ssd"])[0, c]
            m["st_conv"] = f(inputs["state_ssd_conv"])[0, c]
        in_maps.append(m)
    res = run_bass_kernel_spmd(nc, in_maps, core_ids=list(range(NCORES)))
    return res.results, names


def _assemble(r, names, strict):
    cat = lambda nm: np.concatenate([r[c][nm] for c in range(NCORES)], axis=0)
    stk = lambda nm: np.stack([r[c][nm] for c in range(NCORES)], axis=0)

    def get(nm, fn):
        if nm in names:
            return fn()
        if strict:
            raise NotImplementedError(f"output {nm} is not produced by this build")
        return None
    y_prompt = cat("yp")[None]
    y_sample = stk("ys")
    p_ckv = get("p_ckv", lambda: cat("p_ckv")[None, None]); p_kpe = get("p_kpe", lambda: cat("p_kpe")[None, None])
    s_ckv = get("s_ckv", lambda: stk("s_ckv")[None]); s_kpe = get("s_kpe", lambda: stk("s_kpe")[None])
    p_gla = get("p_gla", lambda: r[NCORES - 1]["p_gla"][None, None]); s_gla = get("s_gla", lambda: stk("s_gla")[None])
    p_ssd = get("p_ssd", lambda: r[NCORES - 1]["p_ssd"][None, None]); s_ssd = get("s_ssd", lambda: stk("s_ssd")[None])
    p_conv = get("p_conv", lambda: r[NCORES - 1]["p_conv"][None, None]); s_conv = get("s_conv", lambda: stk("s_conv")[None])
    return (y_prompt, y_sample, p_ckv, p_kpe, p_gla, p_ssd, p_conv, s_ckv, s_kpe, s_gla, s_ssd, s_conv)


def _dev_run(**inputs):
    r, names = _run(eval_stage(), inputs)
    return _assemble(r, names, strict=False)


def eval_stage():
    s = os.environ.get("KSTAGE", "")
    st = {}
    for tok in s.split(","):
        if tok:
            st[tok] = True
    return st
```
